# Optimizing a Trainium2 kernel written in Bass

```python
import jax, jax.numpy as jnp
from jax import lax
import numpy as np

D_MODEL = 2048
BATCH = 16
SEQ = 256
DEPTH = 2
DEC_BATCH = 8
DEC_SEQ = 2048
PAST_LEN = 256

GRID_W = 64
DN_HEADS = 8
DN_DK = 128
DN_DV = 128
DN_QK = DN_HEADS * DN_DK
DN_VW = DN_HEADS * DN_DV
DN_CONV = 5
DN_CHUNK = 64
SGU_WIDTH = 1024
SGU_GROUPS = 4
SGU_CHUNK = 128
POOL_WIDTH = 1024
POOL_GROUPS = 4
POOL_WINDOWS = (2, 4, 8, 16)
D_FF = 5632
FFN_CONV = 3
N_BRANCH = 3
EPS = 1e-6
IN_SIZES = (2 * DN_QK + DN_VW, DN_VW, 2 * DN_HEADS, 2 * DN_HEADS, SGU_WIDTH, SGU_WIDTH, POOL_WIDTH, N_BRANCH * D_MODEL)
D_IN = 2 * DN_QK + 2 * DN_VW + 4 * DN_HEADS + 2 * SGU_WIDTH + POOL_WIDTH + N_BRANCH * D_MODEL

kernel_name = "hybrid_flow_backbone_step"


def rmsnorm(x, w):
    xf = x.astype(jnp.float32)
    y = xf * lax.rsqrt(jnp.mean(xf * xf, axis=-1, keepdims=True) + EPS)
    return (y * w.astype(jnp.float32)).astype(x.dtype)


def l2norm(x):
    return x * lax.rsqrt(jnp.sum(x * x, axis=-1, keepdims=True) + EPS)


def conv1d_centred(x, k):
    return lax.conv_general_dilated(x, k[:, None, :], window_strides=(1,), padding='SAME',
                                    dimension_numbers=('NWC', 'WIO', 'NWC'),
                                    feature_group_count=x.shape[-1])


def depthwise_conv2d(x, k, rows, cols):
    B, L, C = x.shape
    y = lax.conv_general_dilated(x.reshape(B, rows, cols, C), k[:, :, None, :], window_strides=(1, 1),
                                 padding='SAME', dimension_numbers=('NHWC', 'HWIO', 'NHWC'),
                                 feature_group_count=C)
    return y.reshape(B, L, C)


def box_mean_minus_self(x, rows, cols, w):
    B, L, C = x.shape
    xg = x.reshape(B, rows, cols, C).astype(jnp.float32)
    sat = jnp.pad(jnp.cumsum(jnp.cumsum(xg, axis=1), axis=2), ((0, 0), (1, 0), (1, 0), (0, 0)))
    half = w // 2
    ri = jnp.arange(rows)
    ci = jnp.arange(cols)
    r0, r1 = jnp.clip(ri - half, 0, rows), jnp.clip(ri + half, 0, rows)
    c0, c1 = jnp.clip(ci - half, 0, cols), jnp.clip(ci + half, 0, cols)
    corner = lambda r, c: sat[:, r][:, :, c]
    s = corner(r1, c1) - corner(r0, c1) - corner(r1, c0) + corner(r0, c0)
    cnt = ((r1 - r0)[:, None] * (c1 - c0)[None, :]).astype(jnp.float32)
    return (s / cnt[None, :, :, None] - xg).reshape(B, L, C).astype(x.dtype)


def _chunk(t, c):
    B, L, H = t.shape[:3]
    t = t.reshape((B, L // c, c, H) + t.shape[3:])
    return jnp.moveaxis(t, 3, 2)


def gated_delta_rule(q, k, v, g, beta, s0):
    B, L, H, _ = q.shape
    C = DN_CHUNK
    q, k, v = _chunk(q, C), _chunk(k, C), _chunk(v, C)
    g, beta = _chunk(g, C), _chunk(beta, C)
    gc = jnp.cumsum(g, axis=-1)
    lower = jnp.tril(jnp.ones((C, C), bool))
    strict = jnp.tril(jnp.ones((C, C), bool), -1)
    decay = jnp.exp(jnp.where(lower, gc[..., :, None] - gc[..., None, :], -jnp.inf))
    kb = k * beta[..., None]
    a_mat = jnp.where(strict, jnp.einsum('bnhik,bnhjk->bnhij', kb, k) * decay, 0.0)
    rhs = jnp.concatenate([v * beta[..., None], kb * jnp.exp(gc)[..., None]], axis=-1)
    sol = lax.linalg.triangular_solve(jnp.eye(C, dtype=q.dtype) + a_mat, rhs, left_side=True, lower=True)
    u, w = sol[..., :DN_DV], sol[..., DN_DV:]
    qk = jnp.einsum('bnhik,bnhjk->bnhij', q, k) * decay

    def step(s, blk):
        qn, kn, un, wn, gn, qkn = blk
        v_new = un - jnp.einsum('bhck,bhkv->bhcv', wn, s)
        o = (jnp.einsum('bhck,bhkv->bhcv', qn * jnp.exp(gn)[..., None], s)
             + jnp.einsum('bhij,bhjv->bhiv', qkn, v_new))
        g_last = gn[..., -1]
        s = (s * jnp.exp(g_last)[..., None, None]
             + jnp.einsum('bhck,bhcv->bhkv', kn * jnp.exp(g_last[..., None] - gn)[..., None], v_new))
        return s, o

    xs = tuple(jnp.moveaxis(t, 1, 0) for t in (q, k, u, w, gc, qk))
    s_fin, o = lax.scan(step, s0, xs)
    o = jnp.moveaxis(jnp.moveaxis(o, 0, 1), 2, 3).reshape(B, L, H, DN_DV)
    return o, s_fin


def token_mixer(h, lw, s0, rows, cols):
    B, L, _ = h.shape
    f32 = jnp.float32
    splits = np.cumsum(IN_SIZES)[:-1].tolist()
    qkv, z, bg, ag, u_b, v_b, x_c, gates = jnp.split(h @ lw['w_in'], splits, axis=-1)

    qkv = jax.nn.silu(conv1d_centred(qkv, lw['conv_qkv']))
    q, k, v = jnp.split(qkv, [DN_QK, 2 * DN_QK], axis=-1)
    q = l2norm(q.reshape(B, L, DN_HEADS, DN_DK).astype(f32)) * (DN_DK ** -0.5)
    k = l2norm(k.reshape(B, L, DN_HEADS, DN_DK).astype(f32))
    v = v.reshape(B, L, DN_HEADS, DN_DV).astype(f32)
    beta = jax.nn.sigmoid(bg.astype(f32)).reshape(B, L, 2, DN_HEADS)
    g = -jnp.exp(lw['a_log'].astype(f32)) * jax.nn.softplus(
        ag.astype(f32).reshape(B, L, 2, DN_HEADS) + lw['dt_bias'].astype(f32))
    s0 = s0.astype(f32)
    o_f, s_f = gated_delta_rule(q, k, v, g[:, :, 0], beta[:, :, 0], s0[:, 0])
    flip = lambda t: jnp.flip(t, axis=1)
    o_b, s_b = gated_delta_rule(flip(q), flip(k), flip(v), flip(g[:, :, 1]), flip(beta[:, :, 1]), s0[:, 1])
    o = o_f + flip(o_b)
    o = rmsnorm(o, lw['dn_norm_w']) * jax.nn.silu(z.reshape(B, L, DN_HEADS, DN_DV).astype(f32))
    y_a = o.reshape(B, L, DN_VW).astype(h.dtype) @ lw['p_a']
    new_s = jnp.stack([s_f, s_b], axis=1)

    u_b = jax.nn.gelu(u_b)
    v_b = rmsnorm(jax.nn.gelu(v_b), lw['sgu_norm_w'])
    vg = v_b.reshape(B, L // SGU_CHUNK, SGU_CHUNK, SGU_GROUPS, SGU_WIDTH // SGU_GROUPS)
    sp = jnp.einsum('gpq,bnqgc->bnpgc', lw['w_spatial'], vg) + lw['b_spatial'].T[None, None, :, :, None]
    y_b = (u_b * sp.reshape(B, L, SGU_WIDTH)) @ lw['p_b']

    gw = POOL_WIDTH // POOL_GROUPS
    xcg = x_c.reshape(B, L, POOL_GROUPS, gw)
    pooled = jnp.stack([box_mean_minus_self(xcg[:, :, i], rows, cols, win)
                        for i, win in enumerate(POOL_WINDOWS)], axis=2)
    mixed = jnp.einsum('blgc,gcd->blgd', pooled, lw['pool_w']).reshape(B, L, POOL_WIDTH) * lw['pool_scale']
    y_c = mixed @ lw['p_c']

    gt = jax.nn.sigmoid(gates.astype(f32)).reshape(B, L, N_BRANCH, D_MODEL).astype(h.dtype)
    y = gt[:, :, 0] * y_a + gt[:, :, 1] * y_b + gt[:, :, 2] * y_c
    return y @ lw['w_out'], new_s


def conv_ffn(h, w_up, k_conv, w_down, rows, cols):
    up = depthwise_conv2d(h @ w_up, k_conv, rows, cols)
    gate, val = jnp.split(up, 2, axis=-1)
    return (jax.nn.silu(gate) * val) @ w_down


def trunk_layer(x, cond, lw, s0, rows, cols):
    mod = (jax.nn.silu(cond) @ lw['w_ada'] + lw['b_ada'])[:, None, :]
    sh1, sc1, g1, sh2, sc2, g2 = jnp.split(mod, 6, axis=-1)
    h = rmsnorm(x, lw['norm1_w']) * (1 + sc1) + sh1
    y, s = token_mixer(h, lw, s0, rows, cols)
    x = x + g1 * y
    h = rmsnorm(x, lw['norm2_w']) * (1 + sc2) + sh2
    x = x + g2 * conv_ffn(h, lw['w_up'], lw['conv_ffn'], lw['w_down'], rows, cols)
    return x, s


def setup_inputs(seed: int = 0) -> dict:
    key = jax.random.key(seed)
    ks = jax.random.split(key, 32)
    nrm = lambda k, shape, s: jax.random.normal(k, shape, jnp.float32) * s
    gw = POOL_WIDTH // POOL_GROUPS
    w_in = nrm(ks[5], (DEPTH, D_MODEL, D_IN), D_MODEL ** -0.5)
    lo = 2 * DN_QK + 2 * DN_VW
    w_in = w_in.at[:, :, lo:lo + 4 * DN_HEADS].multiply(0.1)
    dt = jnp.exp(jax.random.uniform(ks[7], (DEPTH, 2, DN_HEADS), jnp.float32, np.log(1e-3), np.log(1e-1)))
    return {
        'x_prompt': nrm(ks[0], (BATCH, SEQ, D_MODEL), 1.0),
        'x_sample': nrm(ks[1], (DEC_BATCH, DEC_SEQ, D_MODEL), 1.0),
        'state_delta': nrm(ks[2], (DEC_BATCH, DEPTH, 2, DN_HEADS, DN_DK, DN_DV), 0.3),
        'c': nrm(ks[3], (DEC_BATCH, D_MODEL), 1.0),
        'c_ctx': nrm(ks[4], (D_MODEL,), 1.0),
        'w_ada': nrm(ks[8], (DEPTH, D_MODEL, 6 * D_MODEL), 0.5 * D_MODEL ** -0.5),
        'b_ada': nrm(ks[9], (DEPTH, 6 * D_MODEL), 0.01),
        'norm1_w': 1.0 + nrm(ks[10], (DEPTH, D_MODEL), 0.02),
        'norm2_w': 1.0 + nrm(ks[11], (DEPTH, D_MODEL), 0.02),
        'w_in': w_in,
        'conv_qkv': nrm(ks[12], (DEPTH, DN_CONV, 2 * DN_QK + DN_VW), DN_CONV ** -0.5),
        'a_log': jnp.log(jax.random.uniform(ks[6], (DEPTH, 2, DN_HEADS), jnp.float32, 1.0, 16.0)),
        'dt_bias': dt + jnp.log(-jnp.expm1(-dt)),
        'dn_norm_w': 1.0 + nrm(ks[13], (DEPTH, DN_DV), 0.02),
        'sgu_norm_w': 1.0 + nrm(ks[14], (DEPTH, SGU_WIDTH), 0.02),
        'w_spatial': nrm(ks[15], (DEPTH, SGU_GROUPS, SGU_CHUNK, SGU_CHUNK), SGU_CHUNK ** -0.5),
        'b_spatial': 1.0 + nrm(ks[16], (DEPTH, SGU_GROUPS, SGU_CHUNK), 0.01),
        'pool_w': nrm(ks[17], (DEPTH, POOL_GROUPS, gw, gw), gw ** -0.5),
        'pool_scale': 1.0 + nrm(ks[18], (DEPTH, POOL_WIDTH), 0.1),
        'p_a': nrm(ks[19], (DEPTH, DN_VW, D_MODEL), DN_VW ** -0.5),
        'p_b': nrm(ks[20], (DEPTH, SGU_WIDTH, D_MODEL), SGU_WIDTH ** -0.5),
        'p_c': nrm(ks[21], (DEPTH, POOL_WIDTH, D_MODEL), POOL_WIDTH ** -0.5),
        'w_out': nrm(ks[22], (DEPTH, D_MODEL, D_MODEL), D_MODEL ** -0.5),
        'w_up': nrm(ks[23], (DEPTH, D_MODEL, 2 * D_FF), D_MODEL ** -0.5),
        'conv_ffn': nrm(ks[24], (DEPTH, FFN_CONV, FFN_CONV, 2 * D_FF), 1.0 / FFN_CONV),
        'w_down': nrm(ks[25], (DEPTH, D_FF, D_MODEL), D_FF ** -0.5),
        'final_norm_w': 1.0 + nrm(ks[26], (D_MODEL,), 0.02),
    }


def reference(x_prompt, x_sample, state_delta, c, c_ctx, w_ada, b_ada, norm1_w, norm2_w, w_in, conv_qkv,
              a_log, dt_bias, dn_norm_w, sgu_norm_w, w_spatial, b_spatial, pool_w, pool_scale, p_a, p_b, p_c,
              w_out, w_up, conv_ffn, w_down, final_norm_w):
    bp, lp, _ = x_prompt.shape
    ls = x_sample.shape[1]
    rows = ls // GRID_W
    xp, xs = x_prompt, x_sample
    zero_state = jnp.zeros((bp, 2, DN_HEADS, DN_DK, DN_DV), jnp.float32)
    ctx_states = []
    for l in range(DEPTH):
        lw = {'w_ada': w_ada[l], 'b_ada': b_ada[l], 'norm1_w': norm1_w[l], 'norm2_w': norm2_w[l],
              'w_in': w_in[l], 'conv_qkv': conv_qkv[l], 'a_log': a_log[l], 'dt_bias': dt_bias[l],
              'dn_norm_w': dn_norm_w[l], 'sgu_norm_w': sgu_norm_w[l], 'w_spatial': w_spatial[l],
              'b_spatial': b_spatial[l], 'pool_w': pool_w[l], 'pool_scale': pool_scale[l],
              'p_a': p_a[l], 'p_b': p_b[l], 'p_c': p_c[l], 'w_out': w_out[l],
              'w_up': w_up[l], 'conv_ffn': conv_ffn[l], 'w_down': w_down[l]}
        xp, s_ctx = trunk_layer(xp, c_ctx[None, :], lw, zero_state, 1, lp)
        ctx_states.append(s_ctx)
        xs, _ = trunk_layer(xs, c, lw, state_delta[:, l], rows, GRID_W)
    new_state_delta = jnp.stack(ctx_states, axis=1).astype(x_prompt.dtype)
    y_prompt = rmsnorm(xp, final_norm_w)
    y_sample = rmsnorm(xs, final_norm_w)
    return (y_prompt, y_sample, new_state_delta)
```

```python
import contextlib
import numpy as np
import concourse.bass as bass
import concourse.mybir as mybir
from concourse.bass_utils import run_bass_kernel_spmd

F32 = mybir.dt.float32
BF16 = mybir.dt.bfloat16
AF = mybir.ActivationFunctionType
ALU = mybir.AluOpType

D = 2048
NT = 2560
LS = 2048
LP = 256
NTILE = NT // 128
NTB = NT // 512
DEPTH = 2
DFF = 5632
DIN = 13344
EPS = 1e-6
H = 8
EPOCH = 24000
DN_CH = 128
NLEV = 6
N_DSEM = 4
MAX_CEP = 6
MAX_DEP = 4

DEBUG = False
STOP_AFTER = None


class Tok:
    __slots__ = ("w", "r", "name", "excl")

    def __init__(self, name="", excl=False):
        self.w = None
        self.r = []
        self.name = name
        self.excl = excl


class Op:
    __slots__ = ("eng", "fn", "deps", "is_dma", "signal", "sem", "val")

    def __init__(self, eng, fn, is_dma):
        self.eng = eng
        self.fn = fn
        self.deps = []
        self.is_dma = is_dma
        self.signal = is_dma
        self.sem = None
        self.val = None


class Prog:
    ENGS = ("pe", "act", "dve", "pool", "sp")
    QUEUES = ("sp", "act", "pool")
    COMPUTE = ("pe", "act", "dve", "pool")

    def __init__(self, nc, stack):
        self.nc = nc
        self.ops = {e: [] for e in self.ENGS}
        self.csem = {e: [stack.enter_context(nc.semaphore(f"c_{e}_{i}")) for i in range(MAX_CEP)]
                     for e in self.COMPUTE}
        self.dsem = {q: [[stack.enter_context(nc.semaphore(f"d_{q}_{k}_{i}")) for i in range(MAX_DEP)]
                         for k in range(N_DSEM)] for q in self.QUEUES}
        self.ccnt = {e: 0 for e in self.COMPUTE}
        self.dcnt = {q: [0] * N_DSEM for q in self.QUEUES}
        self.di = {q: 0 for q in self.QUEUES}
        self.waited = {e: {} for e in self.ENGS}
        self.toks = []
        self.out_dmas = []
        self.n_total = 0

    def tok(self, name=""):
        t = Tok(name)
        self.toks.append(t)
        return t

    def _add(self, eng, fn, reads, writes, is_dma):
        op = Op(eng, fn, is_dma)
        pe = (eng == "pe")
        ex = [t for t in reads if t.excl]
        if ex:
            reads = [t for t in reads if not t.excl]
            writes = list(writes) + [t for t in ex if t not in writes]
        seen = set()
        for t in reads:
            d = t.w
            if d is not None and id(d) not in seen and not (pe and d.eng == "pe"):
                seen.add(id(d))
                op.deps.append(d)
                d.signal = True
        for t in writes:
            d = t.w
            if d is not None and id(d) not in seen and not (pe and d.eng == "pe"):
                seen.add(id(d))
                op.deps.append(d)
                d.signal = True
            for d in t.r:
                if id(d) not in seen and d is not op and not (pe and d.eng == "pe"):
                    seen.add(id(d))
                    op.deps.append(d)
                    d.signal = True
        for t in reads:
            t.r.append(op)
        for t in writes:
            t.w = op
            t.r = []
        self.ops[eng].append(op)
        return op

    def op(self, eng, fn, r=(), w=()):
        return self._add(eng, fn, r, w, False)

    def dma(self, eng, fn, r=(), w=()):
        return self._add(eng, fn, r, w, True)

    def flush(self, final=False):
        nc = self.nc
        lasts = []
        for e in self.ENGS:
            lc = None
            nd = 0
            for op in reversed(self.ops[e]):
                if op.is_dma:
                    if nd < N_DSEM:
                        lasts.append(op)
                        nd += 1
                elif lc is None:
                    lc = op
                    op.signal = True
                    lasts.append(op)
                if lc is not None and nd >= N_DSEM:
                    break
        for e in self.ENGS:
            for op in self.ops[e]:
                if op.is_dma:
                    k = self.di[e] % N_DSEM
                    self.di[e] += 1
                    self.dcnt[e][k] += 1
                    c = self.dcnt[e][k]
                    ep = (c * 16) // EPOCH
                    op.sem = self.dsem[e][k][ep]
                    c0 = max(1, -(-(ep * EPOCH) // 16))
                    op.val = (c - c0 + 1) * 16
                elif op.signal:
                    self.ccnt[e] += 1
                    c = self.ccnt[e]
                    ep = c // EPOCH
                    op.sem = self.csem[e][ep]
                    op.val = c - ep * EPOCH + (1 if ep > 0 else 0)
        all_ops = self.ops
        waited = self.waited
        with nc.Block() as block:
            handles = {"pe": block.tensor, "act": block.scalar, "dve": block.vector,
                       "pool": block.gpsimd, "sp": block.sync}

            def make(e):
                def body(eng):
                    wd = waited[e]
                    for op in all_ops[e]:
                        for d in op.deps:
                            key = id(d.sem)
                            if wd.get(key, 0) >= d.val:
                                continue
                            eng.wait_ge(d.sem, d.val)
                            wd[key] = d.val
                        ins = op.fn(eng)
                        if op.sem is not None:
                            ins.then_inc(op.sem, 16 if op.is_dma else 1)
                    for d in lasts:
                        if d.eng == e and not d.is_dma:
                            continue
                        key = id(d.sem)
                        if wd.get(key, 0) >= d.val:
                            continue
                        eng.wait_ge(d.sem, d.val)
                        wd[key] = d.val
                return body
            for e in self.ENGS:
                handles[e](make(e))
        for e in self.ENGS:
            self.n_total += len(self.ops[e])
            self.ops[e] = []
        for t in self.toks:
            t.w = None
            t.r = []


class Buf:
    __slots__ = ("ap", "k")

    def __init__(self, ap, k):
        self.ap = ap
        self.k = k

    def __getitem__(self, idx):
        return self.ap[idx]


def _consts():
    c = {}
    idx = np.arange(128)
    same = (idx[:, None] // DN_CH) == (idx[None, :] // DN_CH)
    c["ident"] = np.eye(128, dtype=np.float32)
    c["ones"] = np.ones((128, 128), np.float32)
    c["blk"] = same.astype(np.float32)
    c["nblk"] = -same.astype(np.float32)
    c["tri_f"] = (same & (idx[:, None] <= idx[None, :])).astype(np.float32)
    c["tri_b"] = (same & (idx[:, None] >= idx[None, :])).astype(np.float32)
    c["neg_f"] = np.where(same & (idx[:, None] >= idx[None, :]), 0.0, -30000.0).astype(np.float32)
    c["neg_b"] = np.where(same & (idx[:, None] <= idx[None, :]), 0.0, -30000.0).astype(np.float32)
    c["str_f"] = (same & (idx[:, None] > idx[None, :])).astype(np.float32)
    c["str_b"] = (same & (idx[:, None] < idx[None, :])).astype(np.float32)
    c["ch0"] = np.ones((128, 128), np.float32)
    c["ch1"] = np.ones((128, 128), np.float32)
    names = ["ident", "ones", "blk", "nblk", "tri_f", "tri_b", "neg_f", "neg_b", "str_f", "str_b", "ch0", "ch1"]
    arr = np.stack([c[n] for n in names], axis=1)
    return names, np.ascontiguousarray(arr)


CONST_NAMES, CONST_ARR = _consts()


def _pool_inv():
    out = np.zeros((4, 32 + 64 + 256), np.float32)
    for i, w in enumerate((2, 4, 8, 16)):
        h = w // 2
        for n, off in ((32, 0), (64, 32), (256, 96)):
            a = np.arange(n)
            cnt = np.clip(a + h, 0, n) - np.clip(a - h, 0, n)
            out[i, off:off + n] = 1.0 / cnt
    return out


POOL_INV = _pool_inv()

_CACHE = {}


def build():
    nc = bass.Bass("TRN2", target_bir_lowering=False)
    stack = contextlib.ExitStack()
    with stack:
        _build(nc, stack)
    return nc


def _build(nc, stack):
    def din(name, shape, dt=F32):
        return nc.dram_tensor(name, list(shape), dt, kind="ExternalInput").ap()

    def dout(name, shape, dt=F32):
        return nc.dram_tensor(name, list(shape), dt, kind="ExternalOutput").ap()

    def dscr(name, shape, dt=F32):
        kind = "ExternalOutput" if DEBUG else "Internal"
        return nc.dram_tensor(name, list(shape), dt, kind=kind).ap()

    x_in = din("x", [NT, D])
    s0_in = din("s0", [DEPTH, 2, H, 128, 128])
    condT_in = din("condT", [128, 16, 2])
    consts_in = din("consts", [128, len(CONST_NAMES), 128])
    poolinv_in = din("poolinv", [1, 1408])
    w_ada = din("w_ada", [DEPTH, D, 6 * D])
    b_adaT = din("b_adaT", [DEPTH, 128, 96])
    n1T = din("n1T", [DEPTH, 128, 16])
    n2T = din("n2T", [DEPTH, 128, 16])
    w_in = din("w_in", [DEPTH, D, DIN])
    convqT = din("convqT", [DEPTH, 128, 24, 5])
    a_log = din("a_log", [DEPTH, 16])
    dt_bias = din("dt_bias", [DEPTH, 16])
    dnwT = din("dnwT", [128, DEPTH])
    sgu_w = din("sgu_w", [DEPTH, 1024])
    wspT = din("wspT", [DEPTH, 128, 4, 128])
    b_sp = din("b_sp", [DEPTH, 512])
    pool_w = din("pool_w", [DEPTH, 4, 256, 256])
    pscT = din("pscT", [DEPTH, 128, 8])
    p_a = din("p_a", [DEPTH, 1024, D])
    p_b = din("p_b", [DEPTH, 1024, D])
    p_c = din("p_c", [DEPTH, 1024, D])
    w_out = din("w_out", [DEPTH, D, D])
    w_up = din("w_up", [DEPTH, D, 2 * DFF])
    cffT = din("cffT", [DEPTH, 128, 88, 9])
    w_down = din("w_down", [DEPTH, DFF, D])
    fnw = din("fnw", [1, D])
    y_out = dout("y", [NT, D])
    ns_out = dout("ns", [2, DEPTH, 2, H, 128, 128])
    xT_d = dscr("xT_d", [D, NT])
    hT_d = dscr("hT_d", [D, NT], BF16)
    gb_d = dscr("gb_d", [NT, 32])
    spT_d = dscr("spT_d", [1024, NT])
    ubspT_d = dscr("ubspT_d", [1024, NT], BF16)
    zsT_d = dscr("zsT_d", [1024, NT])
    poolT_d = dscr("poolT_d", [1024, NT], BF16)
    qkvT_d = dscr("qkvT_d", [3072, NT])
    gatesT_d = dscr("gatesT_d", [3 * D, NT])
    oT_d = dscr("oT_d", [1024, NT], BF16)
    yT_d = dscr("yT_d", [D, NT], BF16)
    actT_d = dscr("actT_d", [DFF, NT], BF16)

    P = Prog(nc, stack)
    banks = []
    for i in range(8):
        t = stack.enter_context(nc.psum_tensor(f"bank{i}", [128, 512], F32))
        banks.append(Buf(t, Tok(f"bank{i}", excl=True)))
    bank_ctr = [0]

    def next_bank(lo=0, hi=8):
        n = hi - lo
        b = banks[lo + bank_ctr[0] % n]
        bank_ctr[0] += 1
        return b

    uniq = [0]

    class Scope:
        def __init__(self):
            self.st = contextlib.ExitStack()
            self.n = 0

        def sb(self, shape, dt=F32, name=None):
            uniq[0] += 1
            nm = f"{name or 't'}_{uniq[0]}"
            t = self.st.enter_context(nc.sbuf_tensor(nm, list(shape), dt))
            return Buf(t, P.tok(nm))

        def close(self, final=False):
            P.flush(final)
            self.st.close()
            for b in banks:
                b.k.w = None
                b.k.r = []

    qrr = [0]

    def ldq():
        qrr[0] += 1
        return "sp" if qrr[0] % 2 else "act"

    G = Scope()
    cst = G.sb([128, len(CONST_NAMES), 128], F32, "cst")
    C = {n: cst[:, i, :] for i, n in enumerate(CONST_NAMES)}
    modT = [G.sb([128, 96, 2], F32, f"modT{l}") for l in range(DEPTH)]
    wv = [[G.sb([128, 16, 2], F32, f"wv{l}_{i}") for i in range(2)] for l in range(DEPTH)]
    epsb = G.sb([128, 1], F32, "epsb")
    P.dma("sp", lambda e: e.dma_start(out=cst[:], in_=consts_in), w=[cst.k])
    P.op("dve", lambda e: e.memset(epsb[:], EPS), w=[epsb.k])

    def stage_done(name):
        return STOP_AFTER is not None and name == STOP_AFTER

    def stage_mod():
        S = Scope()
        ct = S.sb([128, 16, 2], F32, "ct")
        sct = S.sb([128, 16, 2], F32, "sct")
        wts = [S.sb([128, 16, 512], F32, f"wada{i}") for i in range(3)]
        bt = S.sb([128, 96], F32, "bt")
        nt = S.sb([128, 16], F32, "nt")
        mrow = S.sb([2, 6 * D], F32, "mrow")
        P.dma("sp", lambda e: e.dma_start(out=ct[:], in_=condT_in), w=[ct.k])
        P.op("act", lambda e: e.activation(out=sct[:], in_=ct[:], func=AF.Silu), r=[ct.k], w=[sct.k])
        jobs = [(l, jb) for l in range(DEPTH) for jb in range(24)]

        def issue(i):
            l, jb = jobs[i]
            wt = wts[i % 3]
            src = w_ada[l][:, jb * 512:(jb + 1) * 512].rearrange("(kc p) c -> p kc c", p=128)
            P.dma("sp" if i % 2 else "act", lambda e, wt=wt, src=src: e.dma_start(out=wt[:], in_=src), w=[wt.k])
        issue(0)
        issue(1)
        for i, (l, jb) in enumerate(jobs):
            if i + 2 < len(jobs):
                issue(i + 2)
            wt = wts[i % 3]
            bk = next_bank(2, 8)
            for kc in range(16):
                P.op("pe", lambda e, bk=bk, wt=wt, kc=kc: e.matmul(bk[0:2, :], lhsT=sct[:, kc, :], rhs=wt[:, kc, :],
                                                                   start=(kc == 0), stop=(kc == 15)), r=[wt.k, sct.k], w=[bk.k])
            P.op("dve", lambda e, bk=bk, jb=jb: e.tensor_copy(out=mrow[0:2, jb * 512:(jb + 1) * 512], in_=bk[0:2, :]), r=[bk.k], w=[mrow.k])
            if jb == 23:
                bk2 = banks[l]
                for j in range(96):
                    P.op("pe", lambda e, bk2=bk2, j=j: e.transpose(bk2[:, 2 * j:2 * j + 2], mrow[0:2, j * 128:(j + 1) * 128], C["ident"][0:2, 0:2]),
                         r=[mrow.k, cst.k], w=[bk2.k])
                P.dma("sp", lambda e, l=l: e.dma_start(out=bt[:], in_=b_adaT[l]), w=[bt.k])
                for c in range(2):
                    P.op("dve", lambda e, l=l, c=c, bk2=bk2: e.tensor_tensor(
                        out=modT[l][:, :, c], in0=bk2[:, 0:192].rearrange("p (j c) -> p j c", c=2)[:, :, c],
                        in1=bt[:], op=ALU.add), r=[bk2.k, bt.k], w=[modT[l].k])
                for i2, (nT, seg) in enumerate(((n1T, 1), (n2T, 4))):
                    P.dma("sp", lambda e, l=l, nT=nT: e.dma_start(out=nt[:], in_=nT[l]), w=[nt.k])
                    for c in range(2):
                        P.op("dve", lambda e, l=l, i2=i2, c=c, seg=seg: e.scalar_tensor_tensor(
                            out=wv[l][i2][:, :, c], in0=modT[l][:, seg * 16:(seg + 1) * 16, c], scalar=1.0,
                            in1=nt[:], op0=ALU.add, op1=ALU.mult), r=[modT[l].k, nt.k], w=[wv[l][i2].k])
        S.close()

    def stage_t0():
        S = Scope()
        xts = [S.sb([128, D], F32, f"xt{i}") for i in range(2)]
        xos = [S.sb([128, 16, 128], F32, f"xo{i}") for i in range(2)]
        xTv = xT_d.rearrange("(c p) t -> p c t", p=128)
        for t in range(NTILE):
            xt = xts[t % 2]
            xo = xos[t % 2]
            P.dma(ldq(), lambda e, xt=xt, t=t: e.dma_start(out=xt[:], in_=x_in[t * 128:(t + 1) * 128, :]), w=[xt.k])
            for q in range(4):
                bk = next_bank()
                for c4 in range(4):
                    c = q * 4 + c4
                    P.op("pe", lambda e, bk=bk, xt=xt, c=c, c4=c4: e.transpose(
                        bk[:, c4 * 128:(c4 + 1) * 128], xt[:, c * 128:(c + 1) * 128], C["ident"]),
                        r=[xt.k, cst.k], w=[bk.k])
                eng = "act" if q % 2 else "dve"
                if eng == "act":
                    P.op("act", lambda e, bk=bk, xo=xo, q=q: e.copy(
                        out=xo[:, q * 4:(q + 1) * 4, :], in_=bk[:].rearrange("p (a b) -> p a b", a=4)),
                        r=[bk.k], w=[xo.k])
                else:
                    P.op("dve", lambda e, bk=bk, xo=xo, q=q: e.tensor_copy(
                        out=xo[:, q * 4:(q + 1) * 4, :], in_=bk[:].rearrange("p (a b) -> p a b", a=4)),
                        r=[bk.k], w=[xo.k])
            P.dma("sp", lambda e, xo=xo, t=t: e.dma_start(out=xTv[:, :, t * 128:(t + 1) * 128], in_=xo[:]), r=[xo.k])
        S.close()

    def stage_norm(l, which):
        S = Scope()
        xbs = [S.sb([128, 16, 512], F32, f"xb{i}") for i in range(2)]
        sq = S.sb([128, 16, 512], F32, "sq")
        rs = S.sb([128, 512], F32, "rs")
        tmp = [S.sb([128, 512], F32, f"tmp{i}") for i in range(2)]
        hbs = [S.sb([128, 16, 512], BF16, f"hb{i}") for i in range(2)]
        xTv = xT_d.rearrange("(c p) t -> p c t", p=128)
        hTv = hT_d.rearrange("(c p) t -> p c t", p=128)
        shseg = 0 if which == 0 else 3
        wvt = wv[l][which]
        for tb in range(NTB):
            cond = 0 if tb < 4 else 1
            xb = xbs[tb % 2]
            hb = hbs[tb % 2]
            P.dma(ldq(), lambda e, xb=xb, tb=tb: e.dma_start(out=xb[:], in_=xTv[:, :, tb * 512:(tb + 1) * 512]), w=[xb.k])
            P.op("act", lambda e, xb=xb: e.activation(out=sq[:], in_=xb[:], func=AF.Square), r=[xb.k], w=[sq.k])
            bk = next_bank()
            for c in range(16):
                P.op("pe", lambda e, bk=bk, c=c: e.matmul(bk[:], lhsT=C["ones"], rhs=sq[:, c, :], start=(c == 0), stop=(c == 15)),
                     r=[sq.k, cst.k], w=[bk.k])
            P.op("act", lambda e, bk=bk: e.activation(out=rs[:], in_=bk[:], func=AF.Sqrt, scale=1.0 / D, bias=epsb[:]),
                 r=[bk.k, epsb.k], w=[rs.k])
            P.op("dve", lambda e: e.reciprocal(out=rs[:], in_=rs[:]), r=[rs.k], w=[rs.k])
            for c in range(16):
                tm = tmp[c % 2]
                P.op("dve", lambda e, tm=tm, xb=xb, c=c: e.tensor_tensor(out=tm[:], in0=xb[:, c, :], in1=rs[:], op=ALU.mult),
                     r=[xb.k, rs.k], w=[tm.k])
                P.op("act", lambda e, tm=tm, hb=hb, c=c, cond=cond: e.activation(
                    out=hb[:, c, :], in_=tm[:], func=AF.Identity, scale=wvt[:, c, cond:cond + 1],
                    bias=modT[l][:, shseg * 16 + c, cond:cond + 1]), r=[tm.k, wvt.k, modT[l].k], w=[hb.k])
            P.dma("sp", lambda e, hb=hb, tb=tb: e.dma_start(out=hTv[:, :, tb * 512:(tb + 1) * 512], in_=hb[:]), r=[hb.k])
        S.close()

    def wload(dst, src_ap, stg, q="sp", ceng="pool"):
        P.dma(q, lambda e, stg=stg, src_ap=src_ap: e.dma_start(out=stg[:], in_=src_ap), w=[stg.k])
        if ceng == "act":
            P.op("act", lambda e, dst=dst, stg=stg: e.copy(out=dst[:], in_=stg[:]), r=[stg.k], w=[dst.k])
        else:
            P.op(ceng, lambda e, dst=dst, stg=stg: e.tensor_copy(out=dst[:], in_=stg[:]), r=[stg.k], w=[dst.k])

    def gemm_fm(S, inT, KC, wsrcs, epilogue, wbufs, stgs, nbanks=8, tbs=range(NTB)):
        n = len(wsrcs)
        NB = len(wbufs)
        PF = NB - 1

        def issue(j):
            wload(wbufs[j % NB], wsrcs[j].rearrange("(kc p) c -> p kc c", p=128), stgs[j % NB])
        for j in range(min(PF, n)):
            issue(j)
        for j in range(n):
            if j + PF < n:
                issue(j + PF)
            wt = wbufs[j % NB]
            res = []
            for tb in tbs:
                bk = next_bank(0, nbanks)
                for kc in range(KC):
                    P.op("pe", lambda e, bk=bk, wt=wt, kc=kc, tb=tb: e.matmul(
                        bk[:], lhsT=wt[:, kc, :], rhs=inT[:, kc, tb * 512:(tb + 1) * 512],
                        start=(kc == 0), stop=(kc == KC - 1)), r=[wt.k, inT.k], w=[bk.k])
                res.append((tb, bk))
            epilogue(j, res)

    def stage_win(l):
        S = Scope()
        W = w_in[l]
        hin = S.sb([128, 16, NT], BF16, "hin")
        hTv = hT_d.rearrange("(c p) t -> p c t", p=128)
        for tb in range(NTB):
            P.dma(ldq(), lambda e, tb=tb: e.dma_start(out=hin[:, :, tb * 512:(tb + 1) * 512], in_=hTv[:, :, tb * 512:(tb + 1) * 512]),
                  w=[hin.k])
        S1 = Scope()
        wba = S1.sb([128, 16, 32], BF16, "wba")
        nega = S1.sb([128, 16], F32, "nega")
        dtb = S1.sb([128, 16], F32, "dtb")
        gbs = [S1.sb([128, 32], F32, f"gbt{i}") for i in range(2)]
        tt = S1.sb([128, 16], F32, "tt")
        wbas = S1.sb([128, 16, 32], F32, "wbas")
        wload(wba, W[:, 4096:4128].rearrange("(kc p) c -> p kc c", p=128), wbas)
        P.dma("sp", lambda e: e.dma_start(out=nega[:], in_=a_log[l:l + 1, :].partition_broadcast(128)), w=[nega.k])
        P.dma("sp", lambda e: e.dma_start(out=dtb[:], in_=dt_bias[l:l + 1, :].partition_broadcast(128)), w=[dtb.k])
        P.op("act", lambda e: e.activation(out=nega[:], in_=nega[:], func=AF.Exp), r=[nega.k], w=[nega.k])
        P.op("dve", lambda e: e.tensor_scalar(out=nega[:], in0=nega[:], scalar1=-1.0, scalar2=None, op0=ALU.mult), r=[nega.k], w=[nega.k])
        for t in range(NTILE):
            bk = next_bank()
            gbt = gbs[t % 2]
            for kc in range(16):
                P.op("pe", lambda e, bk=bk, kc=kc, t=t: e.matmul(bk[:, 0:32], lhsT=hin[:, kc, t * 128:(t + 1) * 128], rhs=wba[:, kc, :],
                                                                  start=(kc == 0), stop=(kc == 15)), r=[hin.k, wba.k], w=[bk.k])
            P.op("act", lambda e, bk=bk, gbt=gbt: e.activation(out=gbt[:, 0:16], in_=bk[:, 0:16], func=AF.Sigmoid), r=[bk.k], w=[gbt.k])
            P.op("dve", lambda e, bk=bk: e.tensor_tensor(out=tt[:], in0=bk[:, 16:32], in1=dtb[:], op=ALU.add), r=[bk.k, dtb.k], w=[tt.k])
            P.op("act", lambda e: e.activation(out=tt[:], in_=tt[:], func=AF.Exp), r=[tt.k], w=[tt.k])
            P.op("act", lambda e: e.activation(out=tt[:], in_=tt[:], func=AF.Ln, bias=1.0), r=[tt.k], w=[tt.k])
            P.op("dve", lambda e, gbt=gbt: e.tensor_tensor(out=gbt[:, 16:32], in0=tt[:], in1=nega[:], op=ALU.mult), r=[tt.k, nega.k], w=[gbt.k])
            P.dma("sp", lambda e, gbt=gbt, t=t: e.dma_start(out=gb_d[t * 128:(t + 1) * 128, :], in_=gbt[:]), r=[gbt.k])
        S1.close()
        if stage_done("win1"):
            S.close()
            return
        S2 = Scope()
        wvb = S2.sb([128, 16, 1024], BF16, "wvb")
        sgw = S2.sb([128, 1024], F32, "sgw")
        wsp = S2.sb([128, 4, 128], BF16, "wsp")
        bsp = S2.sb([128, 4, 128], F32, "bsp")
        vg = S2.sb([128, 1024], F32, "vg")
        junk = S2.sb([128, 1024], BF16, "junk")
        vn = S2.sb([128, 1024], BF16, "vn")
        ss = S2.sb([128, 2], F32, "ss")
        spts = [S2.sb([128, 8, 128], F32, f"spt{i}") for i in range(2)]
        wvs = S2.sb([128, 16, 512], F32, "wvs")
        for hh in range(2):
            P.dma("sp", lambda e, hh=hh: e.dma_start(
                out=wvs[:], in_=W[:, 5152 + hh * 512:5152 + (hh + 1) * 512].rearrange("(kc p) c -> p kc c", p=128)), w=[wvs.k])
            P.op("pool", lambda e, hh=hh: e.tensor_copy(out=wvb[:, :, hh * 512:(hh + 1) * 512], in_=wvs[:]), r=[wvs.k], w=[wvb.k])
        P.dma("sp", lambda e: e.dma_start(out=sgw[:], in_=sgu_w[l:l + 1, :].partition_broadcast(128)), w=[sgw.k])
        wsps = S2.sb([128, 4, 128], F32, "wsps")
        wload(wsp, wspT[l], wsps)
        P.dma("sp", lambda e: e.dma_start(out=bsp[:].rearrange("p g q -> p (g q)"), in_=b_sp[l:l + 1, :].partition_broadcast(128)), w=[bsp.k])
        spTv = spT_d.rearrange("(j c) t -> c j t", c=128)
        for t in range(NTILE):
            bks = [next_bank(), next_bank()]
            for hh in range(2):
                for kc in range(16):
                    P.op("pe", lambda e, bk=bks[hh], kc=kc, t=t, hh=hh: e.matmul(
                        bk[:], lhsT=hin[:, kc, t * 128:(t + 1) * 128], rhs=wvb[:, kc, hh * 512:(hh + 1) * 512],
                        start=(kc == 0), stop=(kc == 15)), r=[hin.k, wvb.k], w=[bks[hh].k])
                P.op("act", lambda e, bk=bks[hh], hh=hh: e.activation(out=vg[:, hh * 512:(hh + 1) * 512], in_=bk[:], func=AF.Gelu_apprx_tanh),
                     r=[bks[hh].k], w=[vg.k])
            P.op("act", lambda e: e.activation(out=junk[:], in_=vg[:], func=AF.Square, accum_out=ss[:, 0:1]), r=[vg.k], w=[junk.k, ss.k])
            P.op("act", lambda e: e.activation(out=ss[:, 1:2], in_=ss[:, 0:1], func=AF.Sqrt, scale=1.0 / 1024, bias=epsb[:]), r=[ss.k, epsb.k], w=[ss.k])
            P.op("dve", lambda e: e.reciprocal(out=ss[:, 1:2], in_=ss[:, 1:2]), r=[ss.k], w=[ss.k])
            P.op("dve", lambda e: e.scalar_tensor_tensor(out=vn[:], in0=vg[:], scalar=ss[:, 1:2], in1=sgw[:], op0=ALU.mult, op1=ALU.mult),
                 r=[vg.k, ss.k, sgw.k], w=[vn.k])
            spt = spts[t % 2]
            for half in range(2):
                bk = next_bank()
                for i4 in range(4):
                    j = half * 4 + i4
                    g = j // 2
                    P.op("pe", lambda e, bk=bk, i4=i4, j=j, g=g: e.matmul(
                        bk[:, i4 * 128:(i4 + 1) * 128], lhsT=vn[:, j * 128:(j + 1) * 128], rhs=wsp[:, g, :], start=True, stop=True),
                        r=[vn.k, wsp.k], w=[bk.k])
                for gg in range(2):
                    g = half * 2 + gg
                    P.op("dve", lambda e, bk=bk, gg=gg, g=g, spt=spt: e.tensor_tensor(
                        out=spt[:, 2 * g:2 * g + 2, :], in0=bk[:, gg * 256:(gg + 1) * 256].rearrange("p (a b) -> p a b", a=2),
                        in1=bsp[:, g:g + 1, :].to_broadcast([128, 2, 128]), op=ALU.add), r=[bk.k, bsp.k], w=[spt.k])
            P.dma("sp", lambda e, spt=spt, t=t: e.dma_start(out=spTv[:, :, t * 128:(t + 1) * 128], in_=spt[:]), r=[spt.k])
        S2.close()
        if stage_done("win2"):
            S.close()
            return
        S3 = Scope()
        wbufs = [S3.sb([128, 16, 128], BF16, f"wch{i}") for i in range(3)]
        stgs = [S3.sb([128, 16, 128], F32, f"wst{i}") for i in range(3)]

        def cols(c0, n):
            return [W[:, c0 + j * 128:c0 + (j + 1) * 128] for j in range(n)]

        S3a = Scope()
        obufs = [S3a.sb([128, NT], BF16, f"ob{i}") for i in range(2)]
        spb = [S3a.sb([128, NT], F32, f"spb{i}") for i in range(2)]
        ug = S3a.sb([128, 512], F32, "ug")

        def ep_u(j, res):
            ob = obufs[j % 2]
            sp_ = spb[j % 2]
            P.dma(ldq(), lambda e, sp_=sp_, j=j: e.dma_start(out=sp_[:], in_=spT_d[j * 128:(j + 1) * 128, :]), w=[sp_.k])
            for tb, bk in res:
                P.op("act", lambda e, bk=bk: e.activation(out=ug[:], in_=bk[:], func=AF.Gelu_apprx_tanh), r=[bk.k], w=[ug.k])
                P.op("dve", lambda e, ob=ob, sp_=sp_, tb=tb: e.tensor_tensor(out=ob[:, tb * 512:(tb + 1) * 512], in0=ug[:],
                                                                           in1=sp_[:, tb * 512:(tb + 1) * 512], op=ALU.mult),
                     r=[ug.k, sp_.k], w=[ob.k])
            P.dma("sp", lambda e, ob=ob, j=j: e.dma_start(out=ubspT_d[j * 128:(j + 1) * 128, :], in_=ob[:]), r=[ob.k])
        gemm_fm(S3, hin, 16, cols(4128, 8), ep_u, wbufs, stgs)

        fbufs = [S3a.sb([128, NT], F32, f"fb{i}") for i in range(2)]

        def ep_z(j, res):
            fb = fbufs[j % 2]
            for tb, bk in res:
                P.op("act", lambda e, bk=bk, fb=fb, tb=tb: e.activation(out=fb[:, tb * 512:(tb + 1) * 512], in_=bk[:], func=AF.Silu),
                     r=[bk.k], w=[fb.k])
            P.dma("sp", lambda e, fb=fb, j=j: e.dma_start(out=zsT_d[j * 128:(j + 1) * 128, :], in_=fb[:]), r=[fb.k])
        gemm_fm(S3, hin, 16, cols(3072, 8), ep_z, wbufs, stgs)

        def ep_g(j, res):
            fb = fbufs[j % 2]
            for tb, bk in res:
                P.op("act", lambda e, bk=bk, fb=fb, tb=tb: e.activation(out=fb[:, tb * 512:(tb + 1) * 512], in_=bk[:], func=AF.Sigmoid),
                     r=[bk.k], w=[fb.k])
            P.dma("sp", lambda e, fb=fb, j=j: e.dma_start(out=gatesT_d[j * 128:(j + 1) * 128, :], in_=fb[:]), r=[fb.k])
        gemm_fm(S3, hin, 16, cols(7200, 48), ep_g, wbufs, stgs)
        S3a.close()
        if stage_done("win3a"):
            S3.close()
            S.close()
            return
        S3b = Scope()
        fbufs = [S3b.sb([128, NT], F32, f"fb{i}") for i in range(2)]

        cq = S3b.sb([128, 24, 5], F32, "cq")
        P.dma("sp", lambda e: e.dma_start(out=cq[:], in_=convqT[l]), w=[cq.k])
        OFFS = (2, 2 + 2048 + 4, 2 + 2048 + 4 + 256 + 4)
        PADN = NT + 12
        cb = S3b.sb([128, PADN], F32, "cb")
        acc = S3b.sb([128, PADN], F32, "acc")
        sqb = S3b.sb([128, NT], F32, "sqb")
        rn = S3b.sb([128, NT], F32, "rn")
        P.op("dve", lambda e: e.memset(cb[:], 0.0), w=[cb.k])

        def seg_views(buf_ap_fn):
            pass

        def ep_qkv(j, res):
            fb = fbufs[j % 2]
            for tb, bk in res:
                if tb < 4:
                    P.op("act", lambda e, bk=bk, tb=tb: e.copy(out=cb[:, OFFS[0] + tb * 512:OFFS[0] + (tb + 1) * 512], in_=bk[:]),
                         r=[bk.k], w=[cb.k])
                else:
                    for s in range(2):
                        P.op("act", lambda e, bk=bk, s=s: e.copy(out=cb[:, OFFS[1 + s]:OFFS[1 + s] + 256], in_=bk[:, s * 256:(s + 1) * 256]),
                             r=[bk.k], w=[cb.k])
            n = PADN - 4
            for tp in range(5):
                if tp == 0:
                    P.op("dve", lambda e, j=j: e.tensor_scalar(out=acc[:, 2:2 + n], in0=cb[:, 0:n], scalar1=cq[:, j, 0:1], scalar2=None, op0=ALU.mult),
                         r=[cb.k, cq.k], w=[acc.k])
                else:
                    P.op("dve", lambda e, j=j, tp=tp: e.scalar_tensor_tensor(out=acc[:, 2:2 + n], in0=cb[:, tp:tp + n], scalar=cq[:, j, tp:tp + 1],
                                                                          in1=acc[:, 2:2 + n], op0=ALU.mult, op1=ALU.add),
                         r=[cb.k, cq.k, acc.k], w=[acc.k])
            for s, (o, ln, dst) in enumerate(((OFFS[0], 2048, 0), (OFFS[1], 256, 2048), (OFFS[2], 256, 2304))):
                P.op("act", lambda e, fb=fb, o=o, ln=ln, dst=dst: e.activation(out=fb[:, dst:dst + ln], in_=acc[:, o:o + ln], func=AF.Silu),
                     r=[acc.k], w=[fb.k])
            if j < 16:
                P.op("dve", lambda e, fb=fb: e.tensor_tensor(out=sqb[:], in0=fb[:], in1=fb[:], op=ALU.mult), r=[fb.k], w=[sqb.k])
                for tb in range(NTB):
                    bk = next_bank()
                    P.op("pe", lambda e, bk=bk, tb=tb: e.matmul(bk[:], lhsT=C["ones"], rhs=sqb[:, tb * 512:(tb + 1) * 512], start=True, stop=True),
                         r=[sqb.k, cst.k], w=[bk.k])
                    P.op("act", lambda e, bk=bk, tb=tb: e.activation(out=rn[:, tb * 512:(tb + 1) * 512], in_=bk[:], func=AF.Sqrt, bias=epsb[:]),
                         r=[bk.k, epsb.k], w=[rn.k])
                P.op("dve", lambda e: e.reciprocal(out=rn[:], in_=rn[:]), r=[rn.k], w=[rn.k])
                sc = (128.0 ** -0.5) if j < 8 else 1.0
                P.op("dve", lambda e, fb=fb, sc=sc: e.scalar_tensor_tensor(out=fb[:], in0=fb[:], scalar=sc, in1=rn[:], op0=ALU.mult, op1=ALU.mult),
                     r=[fb.k, rn.k], w=[fb.k])
            P.dma("sp", lambda e, fb=fb, j=j: e.dma_start(out=qkvT_d[j * 128:(j + 1) * 128, :], in_=fb[:]), r=[fb.k])
        gemm_fm(S3, hin, 16, cols(0, 24), ep_qkv, wbufs, stgs, nbanks=5)
        S3b.close()
        if stage_done("win3b"):
            S3.close()
            S.close()
            return
        S3c = Scope()
        obufs = [S3c.sb([128, NT], BF16, f"ob{i}") for i in range(2)]

        pinv = S3c.sb([128, 4, 352], F32, "pinv")
        P.dma("sp", lambda e: e.dma_start(out=pinv[:].rearrange("p a b -> p (a b)"),
                                          in_=poolinv_in.partition_broadcast(128)),
              w=[pinv.k])
        PR, PC = 32 + 16, 64 + 16
        pb = S3c.sb([128, PR, PC], F32, "pb")
        t1 = S3c.sb([128, PR, PC], F32, "t1")
        t2 = S3c.sb([128, PR, PC], F32, "t2")
        pp = S3c.sb([128, 2, 256 + 16], F32, "pp")
        u1 = S3c.sb([128, 2, 256 + 16], F32, "u1")
        u2 = S3c.sb([128, 2, 256 + 16], F32, "u2")
        P.op("dve", lambda e: e.memset(pb[:], 0.0), w=[pb.k])
        P.op("dve", lambda e: e.memset(pp[:], 0.0), w=[pp.k])
        P.op("dve", lambda e: e.memset(t1[:], 0.0), w=[t1.k])
        P.op("dve", lambda e: e.memset(t2[:], 0.0), w=[t2.k])
        P.op("dve", lambda e: e.memset(u1[:], 0.0), w=[u1.k])
        P.op("dve", lambda e: e.memset(u2[:], 0.0), w=[u2.k])

        def ep_pool(j, res):
            ob = obufs[j % 2]
            wi = j // 2
            win = (2, 4, 8, 16)[wi]
            hw = win // 2
            nst = wi + 1
            for tb, bk in res:
                if tb < 4:
                    P.op("act", lambda e, bk=bk, tb=tb: e.copy(out=pb[:, 8 + 8 * tb:16 + 8 * tb, 8:72], in_=bk[:].rearrange("p (a b) -> p a b", a=8)),
                         r=[bk.k], w=[pb.k])
                else:
                    P.op("act", lambda e, bk=bk: e.copy(out=pp[:, :, 8:264], in_=bk[:].rearrange("p (a b) -> p a b", a=2)),
                         r=[bk.k], w=[pp.k])
            src, k = pb, 1
            tl = [t1, t2]
            ti = 0
            for st in range(nst):
                dst = tl[ti]
                ncol = PC - 2 * k + 0
                P.op("dve", lambda e, src=src, dst=dst, k=k, ncol=ncol: e.tensor_tensor(
                    out=dst[:, 8:40, 0:ncol], in0=src[:, 8:40, 0:ncol], in1=src[:, 8:40, k:k + ncol], op=ALU.add),
                    r=[src.k], w=[dst.k])
                src = dst
                ti ^= 1
                k *= 2
            cs = tl[ti]
            P.op("dve", lambda e, src=src, cs=cs, hw=hw: e.tensor_copy(out=cs[:, 8:40, 8:72], in_=src[:, 8:40, 8 - hw:72 - hw]), r=[src.k], w=[cs.k])
            src2 = cs
            other = src
            k = 1
            for st in range(nst):
                nrow = PR - 2 * k
                P.op("dve", lambda e, src2=src2, other=other, k=k, nrow=nrow: e.tensor_tensor(
                    out=other[:, 0:nrow, 8:72], in0=src2[:, 0:nrow, 8:72], in1=src2[:, k:k + nrow, 8:72], op=ALU.add),
                    r=[src2.k], w=[other.k])
                src2, other = other, src2
                k *= 2
            P.op("dve", lambda e, src2=src2, other=other, hw=hw, wi=wi: e.tensor_tensor(
                out=other[:, 8:40, 8:72], in0=src2[:, 8 - hw:40 - hw, 8:72],
                in1=pinv[:, wi, 0:32].unsqueeze(2).to_broadcast([128, 32, 64]), op=ALU.mult), r=[src2.k, pinv.k], w=[other.k])
            P.op("dve", lambda e, other=other, wi=wi: e.tensor_tensor(
                out=other[:, 8:40, 8:72], in0=other[:, 8:40, 8:72],
                in1=pinv[:, wi, 32:96].unsqueeze(1).to_broadcast([128, 32, 64]), op=ALU.mult), r=[other.k, pinv.k], w=[other.k])
            P.op("dve", lambda e, other=other, ob=ob: e.tensor_tensor(
                out=ob[:, 0:2048].rearrange("p (a b) -> p a b", a=32), in0=other[:, 8:40, 8:72], in1=pb[:, 8:40, 8:72], op=ALU.subtract),
                r=[other.k, pb.k], w=[ob.k])
            P.op("dve", lambda e: e.memset(t1[:], 0.0), w=[t1.k])
            P.op("dve", lambda e: e.memset(t2[:], 0.0), w=[t2.k])
            srcp, k = pp, 1
            ul = [u1, u2]
            ui = 0
            NP = 256 + 16
            for st in range(nst):
                dst = ul[ui]
                ncol = NP - 2 * k
                P.op("dve", lambda e, srcp=srcp, dst=dst, k=k, ncol=ncol: e.tensor_tensor(
                    out=dst[:, :, 0:ncol], in0=srcp[:, :, 0:ncol], in1=srcp[:, :, k:k + ncol], op=ALU.add), r=[srcp.k], w=[dst.k])
                srcp = dst
                ui ^= 1
                k *= 2
            dstp = ul[ui]
            P.op("dve", lambda e, srcp=srcp, dstp=dstp, hw=hw, wi=wi: e.tensor_tensor(
                out=dstp[:, :, 8:264], in0=srcp[:, :, 8 - hw:264 - hw],
                in1=pinv[:, wi, 96:352].unsqueeze(1).to_broadcast([128, 2, 256]), op=ALU.mult), r=[srcp.k, pinv.k], w=[dstp.k])
            P.op("dve", lambda e, dstp=dstp, ob=ob: e.tensor_tensor(
                out=ob[:, 2048:2560].rearrange("p (a b) -> p a b", a=2), in0=dstp[:, :, 8:264], in1=pp[:, :, 8:264], op=ALU.subtract),
                r=[dstp.k, pp.k], w=[ob.k])
            P.dma("sp", lambda e, ob=ob, j=j: e.dma_start(out=poolT_d[j * 128:(j + 1) * 128, :], in_=ob[:]), r=[ob.k])
        gemm_fm(S3, hin, 16, cols(6176, 8), ep_pool, wbufs, stgs, nbanks=5)
        S3c.close()
        S3.close()
        S.close()

    def stage_dn(l):
        S = Scope()
        seqs = [(0, 16, None), (16, 2, 0), (18, 2, 1)]
        gbt = S.sb([128, NTILE, 32], F32, "gbt")
        GT = S.sb([128, NTILE, 96], F32, "GT")
        gtmp = S.sb([128, 64], F32, "gtmp")
        dnw = S.sb([128, DEPTH], F32, "dnw")
        P.dma("sp", lambda e: e.dma_start(out=gbt[:], in_=gb_d.rearrange("(t p) c -> p t c", p=128)), w=[gbt.k])
        P.dma("sp", lambda e: e.dma_start(out=dnw[:], in_=dnwT), w=[dnw.k])
        for t in range(NTILE):
            bk = next_bank()
            g_all = lambda t=t: gbt[:, t, 16:32]
            P.op("pe", lambda e, bk=bk, t=t: e.matmul(bk[:, 0:8], lhsT=C["tri_f"], rhs=gbt[:, t, 16:24], start=True, stop=True), r=[gbt.k, cst.k], w=[bk.k])
            P.op("pe", lambda e, bk=bk, t=t: e.matmul(bk[:, 8:16], lhsT=C["tri_b"], rhs=gbt[:, t, 24:32], start=True, stop=True), r=[gbt.k, cst.k], w=[bk.k])
            P.op("pe", lambda e, bk=bk, t=t: e.matmul(bk[:, 16:32], lhsT=C["blk"], rhs=gbt[:, t, 16:32], start=True, stop=True), r=[gbt.k, cst.k], w=[bk.k])
            P.op("pe", lambda e, bk=bk, t=t: e.matmul(bk[:, 32:48], lhsT=C["ch0"], rhs=gbt[:, t, 16:32], start=True, stop=True), r=[gbt.k, cst.k], w=[bk.k])
            P.op("pe", lambda e, bk=bk, t=t: e.matmul(bk[:, 48:64], lhsT=C["ch1"], rhs=gbt[:, t, 16:32], start=True, stop=True), r=[gbt.k, cst.k], w=[bk.k])
            P.op("act", lambda e, bk=bk: e.copy(out=gtmp[:], in_=bk[:, 0:64]), r=[bk.k], w=[gtmp.k])
            P.op("act", lambda e, t=t: e.activation(out=GT[:, t, 0:16], in_=gtmp[:, 0:16], func=AF.Exp), r=[gtmp.k], w=[GT.k])
            P.op("dve", lambda e: e.tensor_tensor(out=gtmp[:, 16:32], in0=gtmp[:, 16:32], in1=gtmp[:, 0:16], op=ALU.subtract), r=[gtmp.k], w=[gtmp.k])
            P.op("act", lambda e, t=t: e.activation(out=GT[:, t, 16:64], in_=gtmp[:, 16:64], func=AF.Exp), r=[gtmp.k], w=[GT.k])
            P.op("dve", lambda e, t=t: e.tensor_scalar(out=GT[:, t, 64:80], in0=GT[:, t, 0:16], scalar1=-1.0, scalar2=None, op0=ALU.mult), r=[GT.k], w=[GT.k])
            P.op("dve", lambda e, t=t: e.tensor_scalar(out=GT[:, t, 80:96], in0=gbt[:, t, 0:16], scalar1=-1.0, scalar2=None, op0=ALU.mult), r=[gbt.k], w=[GT.k])

        qT = S.sb([128, NT], F32, "qT")
        kT = S.sb([128, NT], F32, "kT")
        vT = S.sb([128, NT], F32, "vT")
        zsb = [S.sb([128, 128], F32, f"zsb{i}") for i in range(2)]
        ogT = S.sb([128, NT], BF16, "ogT")
        ktok = [S.sb([128, 128], F32, f"ktok{t}") for t in range(NTILE)]
        vtok = [S.sb([128, 128], F32, f"vtok{t}") for t in range(NTILE)]
        TTb = [S.sb([128, 2, 128], F32, f"TTb{t}") for t in range(NTILE)]
        WTt = [S.sb([128, 2, 128], F32, f"WT{t}") for t in range(NTILE)]
        od = [[S.sb([128, 128], F32, f"od{t}_{d}") for d in range(2)] for t in range(NTILE)]
        NCH = 4
        wk = []
        for c in range(NCH):
            wk.append(dict(G1=S.sb([128, 2, 128], F32), Dm=S.sb([128, 2, 128], F32), KKs=S.sb([128, 2, 128], F32),
                           P0=S.sb([128, 2, 128], F32), PT0=S.sb([128, 2, 128], F32), Wm=S.sb([128, 2, 128], F32),
                           Pa=S.sb([128, 2, 128], F32), PTa=S.sb([128, 2, 128], F32),
                           Pb=S.sb([128, 2, 128], F32), PTb=S.sb([128, 2, 128], F32),
                           TTf=S.sb([128, 2, 128], F32)))
        NSC = 6
        Sst = [S.sb([128, 128], F32, f"Sst{i}") for i in range(NSC)]
        rbuf = [S.sb([128, 128], F32, f"rbuf{i}") for i in range(NSC)]
        vnb = [S.sb([128, 128], F32, f"vnb{i}") for i in range(NSC)]
        kdb = [[S.sb([128, 128], F32, f"kdb{i}_{r}") for r in range(2)] for i in range(NSC)]
        osets = [dict(osum=S.sb([128, 128], F32), ojunk=S.sb([128, 128], F32), onrm=S.sb([128, 128], F32), oss=S.sb([128, 2], F32)) for _ in range(2)]

        ci_str = CONST_NAMES.index("str_f")
        STR2 = cst[:, ci_str:ci_str + 2, :]
        trim = (C["tri_f"], C["tri_b"])
        negm = (C["neg_f"], C["neg_b"])
        ID2 = C["ident"].unsqueeze(1).to_broadcast([128, 2, 128])

        for h in range(H):
            for nm, buf, row0 in (("q", qT, h * 128), ("k", kT, 1024 + h * 128), ("v", vT, 2048 + h * 128)):
                P.dma(ldq(), lambda e, buf=buf, row0=row0: e.dma_start(out=buf[:], in_=qkvT_d[row0:row0 + 128, :]), w=[buf.k])
            for t0 in range(0, NTILE, NCH):
                tiles = list(range(t0, t0 + NCH))
                SB = {t: banks[4 + i] for i, t in enumerate(tiles)}
                CB = {t: banks[i] for i, t in enumerate(tiles)}
                WK = {t: wk[i] for i, t in enumerate(tiles)}
                for t in tiles:
                    sl = slice(t * 128, (t + 1) * 128)
                    bk = SB[t]
                    P.op("pe", lambda e, bk=bk, sl=sl: e.transpose(bk[:, 0:128], kT[:, sl], C["ident"]), r=[kT.k, cst.k], w=[bk.k])
                    P.op("pe", lambda e, bk=bk, sl=sl: e.transpose(bk[:, 128:256], vT[:, sl], C["ident"]), r=[vT.k, cst.k], w=[bk.k])
                    P.op("pe", lambda e, bk=bk, sl=sl: e.matmul(bk[:, 256:384], lhsT=kT[:, sl], rhs=kT[:, sl], start=True, stop=True), r=[kT.k], w=[bk.k])
                    P.op("pe", lambda e, bk=bk, sl=sl: e.matmul(bk[:, 384:512], lhsT=qT[:, sl], rhs=kT[:, sl], start=True, stop=True), r=[qT.k, kT.k], w=[bk.k])
                for t in tiles:
                    bk = SB[t]
                    P.op("act", lambda e, bk=bk, t=t: e.copy(out=ktok[t][:], in_=bk[:, 0:128]), r=[bk.k], w=[ktok[t].k])
                    P.op("dve", lambda e, bk=bk, t=t: e.tensor_copy(out=vtok[t][:], in_=bk[:, 128:256]), r=[bk.k], w=[vtok[t].k])
                for t in tiles:
                    w_ = WK[t]
                    for d in range(2):
                        col = d * 8 + h
                        P.op("dve", lambda e, w_=w_, t=t, d=d, col=col: e.tensor_scalar(out=w_["G1"][:, d, :], in0=trim[d], scalar1=gbt[:, t, 16 + col:17 + col], scalar2=None, op0=ALU.mult),
                             r=[gbt.k, cst.k], w=[w_["G1"].k])
                for t in tiles:
                    w_ = WK[t]
                    bk = CB[t]
                    for d in range(2):
                        o = d * 128
                        P.op("pe", lambda e, w_=w_, bk=bk, d=d, o=o: e.matmul(bk[:, o:o + 128], lhsT=w_["G1"][:, d, :], rhs=C["blk"], start=True, stop=False), r=[w_["G1"].k, cst.k], w=[bk.k])
                        P.op("pe", lambda e, w_=w_, bk=bk, d=d, o=o: e.matmul(bk[:, o:o + 128], lhsT=C["nblk"], rhs=w_["G1"][:, d, :], start=False, stop=False), r=[w_["G1"].k, cst.k], w=[bk.k])
                        P.op("pe", lambda e, bk=bk, d=d, o=o: e.matmul(bk[:, o:o + 128], lhsT=C["ident"], rhs=negm[d], start=False, stop=True), r=[cst.k], w=[bk.k])
                for t in tiles:
                    w_ = WK[t]
                    bk = CB[t]
                    sb_ = SB[t]
                    P.op("act", lambda e, w_=w_, bk=bk: e.activation(out=w_["Dm"][:].rearrange("p a b -> p (a b)"), in_=bk[:, 0:256], func=AF.Exp), r=[bk.k], w=[w_["Dm"].k])
                    P.op("dve", lambda e, w_=w_, sb_=sb_: e.tensor_tensor(out=w_["KKs"][:], in0=sb_[:, 256:384].unsqueeze(1).to_broadcast([128, 2, 128]), in1=STR2, op=ALU.mult),
                         r=[sb_.k, cst.k], w=[w_["KKs"].k])
                    for d in range(2):
                        col = d * 8 + h
                        P.op("dve", lambda e, w_=w_, t=t, d=d, col=col: e.scalar_tensor_tensor(out=w_["P0"][:, d, :], in0=w_["Dm"][:, d, :], scalar=GT[:, t, 80 + col:81 + col], in1=w_["KKs"][:, d, :],
                                                                                          op0=ALU.mult, op1=ALU.mult), r=[w_["Dm"].k, GT.k, w_["KKs"].k], w=[w_["P0"].k])
                    P.op("dve", lambda e, w_=w_, sb_=sb_: e.tensor_tensor(out=w_["Wm"][:], in0=sb_[:, 384:512].unsqueeze(1).to_broadcast([128, 2, 128]), in1=w_["Dm"][:], op=ALU.mult),
                         r=[sb_.k, w_["Dm"].k], w=[w_["Wm"].k])
                for t in tiles:
                    w_ = WK[t]
                    bk = CB[t]
                    for d in range(2):
                        P.op("pe", lambda e, w_=w_, bk=bk, d=d: e.transpose(bk[:, d * 128:(d + 1) * 128], w_["P0"][:, d, :], C["ident"]), r=[w_["P0"].k, cst.k], w=[bk.k])
                        P.op("pe", lambda e, w_=w_, bk=bk, d=d: e.transpose(bk[:, 256 + d * 128:256 + (d + 1) * 128], w_["Wm"][:, d, :], C["ident"]), r=[w_["Wm"].k, cst.k], w=[bk.k])
                for t in tiles:
                    w_ = WK[t]
                    bk = CB[t]
                    P.op("act", lambda e, w_=w_, bk=bk: e.copy(out=w_["PT0"][:].rearrange("p a b -> p (a b)"), in_=bk[:, 0:256]), r=[bk.k], w=[w_["PT0"].k])
                    P.op("dve", lambda e, w_=w_, bk=bk: e.tensor_tensor(out=w_["TTf"][:], in0=bk[:, 0:256].rearrange("p (a b) -> p a b", a=2), in1=ID2, op=ALU.add), r=[bk.k, cst.k], w=[w_["TTf"].k])
                    P.op("act", lambda e, bk=bk, t=t: e.copy(out=WTt[t][:].rearrange("p a b -> p (a b)"), in_=bk[:, 256:512]), r=[bk.k], w=[WTt[t].k])
                for s_ in range(1, NLEV + 1):
                    if s_ == 1:
                        pin, ptin = "P0", "PT0"
                    else:
                        pin, ptin = ("Pa", "PTa") if s_ % 2 == 0 else ("Pb", "PTb")
                    pout, ptout = ("Pa", "PTa") if s_ % 2 == 1 else ("Pb", "PTb")
                    for t in tiles:
                        w_ = WK[t]
                        bk = CB[t]
                        for d in range(2):
                            P.op("pe", lambda e, w_=w_, bk=bk, pin=pin, ptin=ptin, d=d: e.matmul(bk[:, d * 128:(d + 1) * 128], lhsT=w_[ptin][:, d, :], rhs=w_[pin][:, d, :], start=True, stop=True),
                                 r=[w_[pin].k, w_[ptin].k], w=[bk.k])
                        if s_ < NLEV:
                            for d in range(2):
                                P.op("pe", lambda e, w_=w_, bk=bk, pin=pin, ptin=ptin, d=d: e.matmul(bk[:, 256 + d * 128:256 + (d + 1) * 128], lhsT=w_[pin][:, d, :], rhs=w_[ptin][:, d, :], start=True, stop=True),
                                     r=[w_[pin].k, w_[ptin].k], w=[bk.k])
                    for t in tiles:
                        w_ = WK[t]
                        bk = CB[t]
                        P.op("act", lambda e, w_=w_, bk=bk, pout=pout: e.copy(out=w_[pout][:].rearrange("p a b -> p (a b)"), in_=bk[:, 0:256]), r=[bk.k], w=[w_[pout].k])
                        if s_ < NLEV:
                            P.op("dve", lambda e, w_=w_, bk=bk, ptout=ptout: e.tensor_copy(out=w_[ptout][:].rearrange("p a b -> p (a b)"), in_=bk[:, 256:512]), r=[bk.k], w=[w_[ptout].k])
                    for t in tiles:
                        w_ = WK[t]
                        bk = CB[t]
                        for d in range(2):
                            P.op("pe", lambda e, w_=w_, bk=bk, pout=pout, d=d: e.matmul(bk[:, d * 128:(d + 1) * 128], lhsT=w_[pout][:, d, :], rhs=w_["TTf"][:, d, :], start=True, stop=True),
                                 r=[w_[pout].k, w_["TTf"].k], w=[bk.k])
                    for t in tiles:
                        w_ = WK[t]
                        bk = CB[t]
                        P.op("dve", lambda e, w_=w_, bk=bk: e.tensor_tensor(out=w_["TTf"][:], in0=bk[:, 0:256].rearrange("p (a b) -> p a b", a=2), in1=w_["TTf"][:], op=ALU.add),
                             r=[bk.k, w_["TTf"].k], w=[w_["TTf"].k])
                for t in tiles:
                    w_ = WK[t]
                    for d in range(2):
                        col = d * 8 + h
                        P.op("act", lambda e, w_=w_, t=t, d=d, col=col: e.activation(out=TTb[t][:, d, :], in_=w_["TTf"][:, d, :], func=AF.Copy, scale=gbt[:, t, col:col + 1]),
                             r=[w_["TTf"].k, gbt.k], w=[TTb[t].k])
            def emit_out(t, h=h):
                sl = slice(t * 128, (t + 1) * 128)
                o_ = osets[t % 2]
                osum, ojunk, onrm, oss = o_["osum"], o_["ojunk"], o_["onrm"], o_["oss"]
                P.op("dve", lambda e: e.tensor_tensor(out=osum[:], in0=od[t][0][:], in1=od[t][1][:], op=ALU.add), r=[od[t][0].k, od[t][1].k], w=[osum.k])
                P.op("act", lambda e: e.activation(out=ojunk[:], in_=osum[:], func=AF.Square, accum_out=oss[:, 0:1]), r=[osum.k], w=[ojunk.k, oss.k])
                P.op("act", lambda e: e.activation(out=oss[:, 1:2], in_=oss[:, 0:1], func=AF.Sqrt, scale=1.0 / 128, bias=epsb[:]), r=[oss.k, epsb.k], w=[oss.k])
                P.op("dve", lambda e: e.reciprocal(out=oss[:, 1:2], in_=oss[:, 1:2]), r=[oss.k], w=[oss.k])
                P.op("dve", lambda e: e.tensor_scalar(out=onrm[:], in0=osum[:], scalar1=oss[:, 1:2], scalar2=None, op0=ALU.mult), r=[osum.k, oss.k], w=[onrm.k])
                bk = banks[4 + t % 2]
                zs = zsb[t % 2]
                P.dma(ldq(), lambda e: e.dma_start(out=zs[:], in_=zsT_d[h * 128:(h + 1) * 128, sl]), w=[zs.k])
                P.op("pe", lambda e: e.transpose(bk[:, 0:128], onrm[:], C["ident"]), r=[onrm.k, cst.k], w=[bk.k])
                P.op("dve", lambda e: e.scalar_tensor_tensor(out=ogT[:, sl], in0=bk[:, 0:128], scalar=dnw[:, l:l + 1], in1=zs[:], op0=ALU.mult, op1=ALU.mult),
                     r=[bk.k, dnw.k, zs.k], w=[ogT.k])

            sch = []
            for si, (tile0, ntl, pidx) in enumerate(seqs):
                for d in range(2):
                    sch.append((si, d))
            for ci, (si, d) in enumerate(sch):
                tile0, ntl, pidx = seqs[si]
                if pidx is None:
                    P.dma(ldq(), lambda e, ci=ci, d=d, h=h: e.dma_start(out=Sst[ci][:], in_=s0_in[l, d, h]), w=[Sst[ci].k])
                else:
                    P.op("dve", lambda e, ci=ci: e.memset(Sst[ci][:], 0.0), w=[Sst[ci].k])
            maxsteps = 16
            for n in range(maxsteps):
                active = []
                for ci, (si, d) in enumerate(sch):
                    tile0, ntl, pidx = seqs[si]
                    nchunks = ntl
                    if n >= nchunks:
                        continue
                    cidx = n if d == 0 else nchunks - 1 - n
                    t = tile0 + cidx
                    hf = 0
                    active.append((ci, si, d, t, hf))

                def bank_of(ci):
                    return banks[6 + ci] if ci < 2 else banks[ci - 2]
                for (ci, si, d, t, hf) in active:
                    rows = slice(0, 128)
                    col = d * 8 + h
                    kd_ = kdb[ci][n % 2]
                    P.op("pool", lambda e, rows=rows, kd_=kd_, t=t, col=col: e.tensor_scalar(out=kd_[rows, :], in0=ktok[t][rows, :], scalar1=GT[rows, t, 16 + col:17 + col], scalar2=None, op0=ALU.mult),
                         r=[ktok[t].k, GT.k], w=[kd_.k])
                for (ci, si, d, t, hf) in active:
                    bk = bank_of(ci)
                    sl = slice(t * 128, (t + 1) * 128)
                    P.op("pe", lambda e, bk=bk, sl=sl, ci=ci: e.matmul(bk[:, 0:128], lhsT=kT[:, sl], rhs=Sst[ci][:], start=True, stop=True), r=[kT.k, Sst[ci].k], w=[bk.k])
                    P.op("pe", lambda e, bk=bk, sl=sl, ci=ci: e.matmul(bk[:, 256:384], lhsT=qT[:, sl], rhs=Sst[ci][:], start=True, stop=True), r=[qT.k, Sst[ci].k], w=[bk.k])
                for (ci, si, d, t, hf) in active:
                    bk = bank_of(ci)
                    rows = slice(0, 128)
                    col = d * 8 + h
                    P.op("dve", lambda e, bk=bk, rows=rows, ci=ci, t=t, col=col: e.scalar_tensor_tensor(
                        out=rbuf[ci][rows, :], in0=bk[rows, 0:128], scalar=GT[rows, t, 64 + col:65 + col], in1=vtok[t][rows, :], op0=ALU.mult, op1=ALU.add),
                        r=[bk.k, GT.k, vtok[t].k], w=[rbuf[ci].k])
                for (ci, si, d, t, hf) in active:
                    bk = bank_of(ci)
                    rows = slice(0, 128)
                    P.op("pe", lambda e, bk=bk, rows=rows, ci=ci, t=t, d=d: e.matmul(bk[:, 128:256], lhsT=TTb[t][rows, d, :], rhs=rbuf[ci][rows, :], start=True, stop=True),
                         r=[TTb[t].k, rbuf[ci].k], w=[bk.k])
                for (ci, si, d, t, hf) in active:
                    bk = bank_of(ci)
                    rows = slice(0, 128)
                    P.op("act", lambda e, bk=bk, rows=rows, ci=ci: e.copy(out=vnb[ci][rows, :], in_=bk[rows, 128:256]), r=[bk.k], w=[vnb[ci].k])
                for (ci, si, d, t, hf) in active:
                    bk = bank_of(ci)
                    rows = slice(0, 128)
                    kd_ = kdb[ci][n % 2]
                    P.op("pe", lambda e, bk=bk, rows=rows, ci=ci, t=t, d=d: e.matmul(bk[:, 384:512], lhsT=WTt[t][rows, d, :], rhs=vnb[ci][rows, :], start=True, stop=True),
                         r=[WTt[t].k, vnb[ci].k], w=[bk.k])
                    P.op("pe", lambda e, bk=bk, rows=rows, ci=ci, kd_=kd_: e.matmul(bk[:, 128:256], lhsT=kd_[rows, :], rhs=vnb[ci][rows, :], start=True, stop=True),
                         r=[kd_.k, vnb[ci].k], w=[bk.k])
                for (ci, si, d, t, hf) in active:
                    bk = bank_of(ci)
                    rows = slice(0, 128)
                    col = d * 8 + h
                    P.op("act", lambda e, bk=bk, rows=rows, t=t, d=d, col=col: e.activation(out=od[t][d][rows, :], in_=bk[rows, 256:384], func=AF.Copy, scale=GT[rows, t, col:col + 1]),
                         r=[bk.k, GT.k], w=[od[t][d].k])
                    P.op("dve", lambda e, bk=bk, rows=rows, t=t, d=d: e.tensor_tensor(out=od[t][d][rows, :], in0=od[t][d][rows, :], in1=bk[rows, 384:512], op=ALU.add),
                         r=[bk.k, od[t][d].k], w=[od[t][d].k])
                    P.op("dve", lambda e, bk=bk, ci=ci, t=t, hf=hf, col=col: e.scalar_tensor_tensor(
                        out=Sst[ci][:], in0=Sst[ci][:], scalar=GT[:, t, 32 + 16 * hf + col:33 + 16 * hf + col], in1=bk[:, 128:256], op0=ALU.mult, op1=ALU.add),
                        r=[Sst[ci].k, GT.k, bk.k], w=[Sst[ci].k])
                done = [t for t in range(16) if max(t, 15 - t) == n]
                if n == 1:
                    done += [16, 17, 18, 19]
                for t in done:
                    emit_out(t)
            for ci, (si, d) in enumerate(sch):
                tile0, ntl, pidx = seqs[si]
                if pidx is not None:
                    op_ = P.dma("sp", lambda e, ci=ci, pidx=pidx, d=d, h=h: e.dma_start(out=ns_out[pidx, l, d, h], in_=Sst[ci][:]), r=[Sst[ci].k])
                    P.out_dmas.append(op_)
            P.dma("sp", lambda e, h=h: e.dma_start(out=oT_d[h * 128:(h + 1) * 128, :], in_=ogT[:]), r=[ogT.k])
        S.close()

    def xres_epilogue(S, l, gseg, xbs):
        def ep(j, res):
            for tb, bk in res:
                cond = 0 if tb < 4 else 1
                xb = xbs[(j * NTB + tb) % len(xbs)]
                P.dma("act", lambda e, xb=xb, j=j, tb=tb: e.dma_start(out=xb[:], in_=xT_d[j * 128:(j + 1) * 128, tb * 512:(tb + 1) * 512]), w=[xb.k])
                P.op("dve", lambda e, xb=xb, bk=bk, j=j, cond=cond: e.scalar_tensor_tensor(
                    out=xb[:], in0=bk[:], scalar=modT[l][:, gseg * 16 + j, cond:cond + 1], in1=xb[:], op0=ALU.mult, op1=ALU.add),
                    r=[bk.k, modT[l].k, xb.k], w=[xb.k])
                P.dma("sp", lambda e, xb=xb, j=j, tb=tb: e.dma_start(out=xT_d[j * 128:(j + 1) * 128, tb * 512:(tb + 1) * 512], in_=xb[:]), r=[xb.k])
        return ep

    def stage_mix(l):
        S = Scope()
        inA = S.sb([128, 8, NT], BF16, "inA")
        inB = S.sb([128, 8, NT], BF16, "inB")
        inC = S.sb([128, 8, NT], BF16, "inC")
        psc = S.sb([128, 8], F32, "psc")
        S1 = Scope()
        plT = S1.sb([128, 8, NT], BF16, "plT")
        for (buf, src) in ((inA, oT_d), (inB, ubspT_d), (plT, poolT_d)):
            v = src.rearrange("(c p) t -> p c t", p=128)
            for tb in range(NTB):
                P.dma(ldq(), lambda e, buf=buf, v=v, tb=tb: e.dma_start(out=buf[:, :, tb * 512:(tb + 1) * 512], in_=v[:, :, tb * 512:(tb + 1) * 512]), w=[buf.k])
        P.dma("sp", lambda e: e.dma_start(out=psc[:], in_=pscT[l]), w=[psc.k])
        wpl = [S1.sb([128, 2, 128], BF16, f"wpl{i}") for i in range(2)]
        wpls = [S1.sb([128, 2, 128], F32, f"wpls{i}") for i in range(2)]
        for g in range(4):
            for oc in range(2):
                j = g * 2 + oc
                wt = wpl[j % 2]
                wload(wt, pool_w[l, g][:, oc * 128:(oc + 1) * 128].rearrange("(kc p) c -> p kc c", p=128), wpls[j % 2])
                for tb in range(NTB):
                    bk = next_bank()
                    for kc in range(2):
                        P.op("pe", lambda e, bk=bk, wt=wt, kc=kc, g=g, tb=tb: e.matmul(bk[:], lhsT=wt[:, kc, :], rhs=plT[:, g * 2 + kc, tb * 512:(tb + 1) * 512],
                                                                                 start=(kc == 0), stop=(kc == 1)), r=[wt.k, plT.k], w=[bk.k])
                    P.op("act", lambda e, bk=bk, j=j, tb=tb: e.activation(out=inC[:, j, tb * 512:(tb + 1) * 512], in_=bk[:], func=AF.Copy, scale=psc[:, j:j + 1]),
                         r=[bk.k, psc.k], w=[inC.k])
        S1.close()
        S2 = Scope()
        wab = [[S2.sb([128, 8, 128], BF16, f"wab{i}_{b}") for b in range(3)] for i in range(2)]
        wabs = [[S2.sb([128, 8, 128], F32, f"wabs{i}_{b}") for b in range(3)] for i in range(2)]
        gts = [S2.sb([128, 3, 512], F32, f"gts{i}") for i in range(2)]
        m0 = S2.sb([128, 512], F32, "m0")
        m1 = S2.sb([128, 512], F32, "m1")
        yb = [S2.sb([128, NT], BF16, f"yb{i}") for i in range(2)]
        gv = gatesT_d.rearrange("(b f) t -> f b t", b=3)

        def issue_w(f):
            for b, pw in enumerate((p_a, p_b, p_c)):
                wload(wab[f % 2][b], pw[l][:, f * 128:(f + 1) * 128].rearrange("(kc p) c -> p kc c", p=128), wabs[f % 2][b])
        issue_w(0)
        for f in range(16):
            if f + 1 < 16:
                issue_w(f + 1)
            ws = wab[f % 2]
            y_ = yb[f % 2]
            for tb in range(NTB):
                gt = gts[(f * NTB + tb) % 2]
                P.dma("act", lambda e, gt=gt, f=f, tb=tb: e.dma_start(out=gt[:], in_=gv[f * 128:(f + 1) * 128, :, tb * 512:(tb + 1) * 512]), w=[gt.k])
                bks = []
                for b, inn in enumerate((inA, inB, inC)):
                    bk = next_bank()
                    for kc in range(8):
                        P.op("pe", lambda e, bk=bk, w_=ws[b], inn=inn, kc=kc, tb=tb: e.matmul(bk[:], lhsT=w_[:, kc, :], rhs=inn[:, kc, tb * 512:(tb + 1) * 512],
                                                                                        start=(kc == 0), stop=(kc == 7)), r=[ws[b].k, inn.k], w=[bk.k])
                    bks.append(bk)
                P.op("dve", lambda e, gt=gt, bk=bks[0]: e.tensor_tensor(out=m0[:], in0=bk[:], in1=gt[:, 0, :], op=ALU.mult), r=[bks[0].k, gt.k], w=[m0.k])
                P.op("dve", lambda e, gt=gt, bk=bks[1]: e.tensor_tensor(out=m1[:], in0=bk[:], in1=gt[:, 1, :], op=ALU.mult), r=[bks[1].k, gt.k], w=[m1.k])
                P.op("dve", lambda e: e.tensor_tensor(out=m0[:], in0=m0[:], in1=m1[:], op=ALU.add), r=[m0.k, m1.k], w=[m0.k])
                P.op("dve", lambda e, gt=gt, bk=bks[2]: e.tensor_tensor(out=m1[:], in0=bk[:], in1=gt[:, 2, :], op=ALU.mult), r=[bks[2].k, gt.k], w=[m1.k])
                P.op("dve", lambda e, y_=y_, tb=tb: e.tensor_tensor(out=y_[:, tb * 512:(tb + 1) * 512], in0=m0[:], in1=m1[:], op=ALU.add), r=[m0.k, m1.k], w=[y_.k])
            P.dma("sp", lambda e, y_=y_, f=f: e.dma_start(out=yT_d[f * 128:(f + 1) * 128, :], in_=y_[:]), r=[y_.k])
        S2.close()
        S.close()
        if stage_done("mix2"):
            return
        S = Scope()
        yin = S.sb([128, 16, NT], BF16, "yin")
        v = yT_d.rearrange("(c p) t -> p c t", p=128)
        for tb in range(NTB):
            P.dma(ldq(), lambda e, tb=tb: e.dma_start(out=yin[:, :, tb * 512:(tb + 1) * 512], in_=v[:, :, tb * 512:(tb + 1) * 512]), w=[yin.k])
        wbufs = [S.sb([128, 16, 128], BF16, f"wo{i}") for i in range(3)]
        stgs = [S.sb([128, 16, 128], F32, f"wos{i}") for i in range(3)]
        xbs = [S.sb([128, 512], F32, f"xr{i}") for i in range(4)]
        gemm_fm(S, yin, 16, [w_out[l][:, j * 128:(j + 1) * 128] for j in range(16)], xres_epilogue(S, l, 2, xbs), wbufs, stgs)
        S.close()

    def stage_ffn_up(l):
        S = Scope()
        hin = S.sb([128, 16, NT], BF16, "hin")
        hTv = hT_d.rearrange("(c p) t -> p c t", p=128)
        for tb in range(NTB):
            P.dma(ldq(), lambda e, tb=tb: e.dma_start(out=hin[:, :, tb * 512:(tb + 1) * 512], in_=hTv[:, :, tb * 512:(tb + 1) * 512]), w=[hin.k])
        cf = S.sb([128, 88, 9], F32, "cf")
        P.dma("sp", lambda e: e.dma_start(out=cf[:], in_=cffT[l]), w=[cf.k])
        wbufs = [S.sb([128, 16, 128], BF16, f"wu{i}") for i in range(3)]
        stgs = [S.sb([128, 16, 128], F32, f"wus{i}") for i in range(3)]
        pbs = [S.sb([128, 34, 66], F32, "pbs0"), S.sb([128, 34, 66], BF16, "pbs1")]
        pps = [S.sb([128, 2, 258], F32, "pps0"), S.sb([128, 2, 258], BF16, "pps1")]
        gacc = S.sb([128, NT], F32, "gacc")
        dgs = [S.sb([128, 9, 128], BF16, f"dg{i}") for i in range(2)]
        identb = S.sb([128, 128], BF16, "identb")
        sg = S.sb([128, NT], F32, "sg")
        obs = [S.sb([128, NT], BF16, f"aob{i}") for i in range(2)]
        P.op("dve", lambda e: e.tensor_copy(out=identb[:], in_=C["ident"]), r=[cst.k], w=[identb.k])
        for i in range(2):
            P.op("dve", lambda e, i=i: e.memset(pbs[i][:], 0.0), w=[pbs[i].k])
            P.op("dve", lambda e, i=i: e.memset(pps[i][:], 0.0), w=[pps[i].k])
        W = w_up[l]
        srcs = []
        for c in range(44):
            srcs.append(W[:, c * 128:(c + 1) * 128])
            srcs.append(W[:, DFF + c * 128:DFF + (c + 1) * 128])

        def conv_finish(j):
            c, gi = j // 2, j % 2
            ch = c if gi == 0 else 44 + c
            pb, pp = pbs[gi], pps[gi]
            if gi == 0:
                accv = gacc[:, 0:2048].rearrange("p (a b) -> p a b", a=32)
                accp = gacc[:, 2048:2560].rearrange("p (a b) -> p a b", a=2)
                for tp in range(9):
                    kh, kw = tp // 3, tp % 3
                    if tp == 0:
                        P.op("dve", lambda e, kh=kh, kw=kw, tp=tp: e.tensor_scalar(out=accv, in0=pb[:, kh:kh + 32, kw:kw + 64], scalar1=cf[:, ch, tp:tp + 1], scalar2=None, op0=ALU.mult),
                             r=[pb.k, cf.k], w=[gacc.k])
                    else:
                        P.op("dve", lambda e, kh=kh, kw=kw, tp=tp: e.scalar_tensor_tensor(out=accv, in0=pb[:, kh:kh + 32, kw:kw + 64], scalar=cf[:, ch, tp:tp + 1], in1=accv, op0=ALU.mult, op1=ALU.add),
                             r=[pb.k, cf.k, gacc.k], w=[gacc.k])
                for kw in range(3):
                    tp = 3 + kw
                    if kw == 0:
                        P.op("dve", lambda e, kw=kw, tp=tp: e.tensor_scalar(out=accp, in0=pp[:, :, kw:kw + 256], scalar1=cf[:, ch, tp:tp + 1], scalar2=None, op0=ALU.mult), r=[pp.k, cf.k], w=[gacc.k])
                    else:
                        P.op("dve", lambda e, kw=kw, tp=tp: e.scalar_tensor_tensor(out=accp, in0=pp[:, :, kw:kw + 256], scalar=cf[:, ch, tp:tp + 1], in1=accp, op0=ALU.mult, op1=ALU.add),
                             r=[pp.k, cf.k, gacc.k], w=[gacc.k])
                P.op("act", lambda e: e.activation(out=sg[:], in_=gacc[:], func=AF.Silu), r=[gacc.k], w=[sg.k])
                return
            dg = dgs[c % 2]
            P.op("pool", lambda e, dg=dg, ch=ch: e.tensor_tensor(
                out=dg[:], in0=identb[:].unsqueeze(1).to_broadcast([128, 9, 128]),
                in1=cf[:, ch, :].unsqueeze(2).to_broadcast([128, 9, 128]), op=ALU.mult), r=[identb.k, cf.k], w=[dg.k])
            ob = obs[c % 2]
            for tb in range(NTB):
                bk = next_bank()
                if tb < 4:
                    for tp in range(9):
                        kh, kw = tp // 3, tp % 3
                        P.op("pe", lambda e, bk=bk, dg=dg, pb=pb, tp=tp, kh=kh, kw=kw, tb=tb: e.matmul(
                            bk[:].rearrange("p (a b) -> p a b", a=8), lhsT=dg[:, tp, :],
                            rhs=pb[:, 8 * tb + kh:8 * tb + kh + 8, kw:kw + 64], start=(tp == 0), stop=(tp == 8)),
                            r=[dg.k, pb.k], w=[bk.k])
                else:
                    for kw in range(3):
                        P.op("pe", lambda e, bk=bk, dg=dg, pp=pp, kw=kw: e.matmul(
                            bk[:].rearrange("p (a b) -> p a b", a=2), lhsT=dg[:, 3 + kw, :],
                            rhs=pp[:, :, kw:kw + 256], start=(kw == 0), stop=(kw == 2)),
                            r=[dg.k, pp.k], w=[bk.k])
                P.op("dve", lambda e, bk=bk, tb=tb, ob=ob: e.tensor_tensor(out=ob[:, tb * 512:(tb + 1) * 512], in0=bk[:], in1=sg[:, tb * 512:(tb + 1) * 512], op=ALU.mult),
                     r=[bk.k, sg.k], w=[ob.k])
            P.dma("sp", lambda e, ob=ob, c=c: e.dma_start(out=actT_d[c * 128:(c + 1) * 128, :], in_=ob[:]), r=[ob.k])

        def ep_up(j, res):
            gi = j % 2
            pb, pp = pbs[gi], pps[gi]
            for tb, bk in res:
                if tb < 4:
                    P.op("act", lambda e, bk=bk, pb=pb, tb=tb: e.copy(out=pb[:, 1 + 8 * tb:9 + 8 * tb, 1:65], in_=bk[:].rearrange("p (a b) -> p a b", a=8)),
                         r=[bk.k], w=[pb.k])
                else:
                    P.op("act", lambda e, bk=bk, pp=pp: e.copy(out=pp[:, :, 1:257], in_=bk[:].rearrange("p (a b) -> p a b", a=2)), r=[bk.k], w=[pp.k])
            if j >= 1:
                conv_finish(j - 1)
        gemm_fm(S, hin, 16, srcs, ep_up, wbufs, stgs)
        conv_finish(87)
        S.close()

    def stage_ffn_down(l):
        for half in range(2):
            S = Scope()
            ain = S.sb([128, 22, NT], BF16, "ain")
            v = actT_d[half * 2816:(half + 1) * 2816, :].rearrange("(c p) t -> p c t", p=128)
            for tb in range(NTB):
                P.dma(ldq(), lambda e, tb=tb, v=v: e.dma_start(out=ain[:, :, tb * 512:(tb + 1) * 512], in_=v[:, :, tb * 512:(tb + 1) * 512]), w=[ain.k])
            wbufs = [S.sb([128, 22, 128], BF16, f"wd{i}") for i in range(3)]
            stgs = [S.sb([128, 22, 128], F32, f"wds{i}") for i in range(3)]
            xbs = [S.sb([128, 512], F32, f"xr{i}") for i in range(4)]
            srcs = [w_down[l][half * 2816:(half + 1) * 2816, j * 128:(j + 1) * 128] for j in range(16)]
            gemm_fm(S, ain, 22, srcs, xres_epilogue(S, l, 5, xbs), wbufs, stgs)
            S.close()

    def stage_final():
        S = Scope()
        fw = S.sb([128, D], F32, "fw")
        P.dma("sp", lambda e: e.dma_start(out=fw[:], in_=fnw.partition_broadcast(128)), w=[fw.k])
        xis = [S.sb([128, 16, 128], F32, f"xi{i}") for i in range(2)]
        xts = [S.sb([128, D], F32, f"xtk{i}") for i in range(2)]
        yts = [S.sb([128, D], F32, f"ytk{i}") for i in range(2)]
        junk = S.sb([128, D], BF16, "fjunk")
        ss = S.sb([128, 2], F32, "fss")
        xTv = xT_d.rearrange("(c p) t -> p c t", p=128)
        for t in range(NTILE):
            xi = xis[t % 2]
            xt = xts[t % 2]
            yt = yts[t % 2]
            P.dma(ldq(), lambda e, xi=xi, t=t: e.dma_start(out=xi[:], in_=xTv[:, :, t * 128:(t + 1) * 128]), w=[xi.k])
            for q in range(4):
                bk = next_bank()
                for c4 in range(4):
                    c = q * 4 + c4
                    P.op("pe", lambda e, bk=bk, xi=xi, c=c, c4=c4: e.transpose(bk[:, c4 * 128:(c4 + 1) * 128], xi[:, c, :], C["ident"]), r=[xi.k, cst.k], w=[bk.k])
                if q % 2:
                    P.op("act", lambda e, bk=bk, xt=xt, q=q: e.copy(out=xt[:, q * 512:(q + 1) * 512], in_=bk[:]), r=[bk.k], w=[xt.k])
                else:
                    P.op("dve", lambda e, bk=bk, xt=xt, q=q: e.tensor_copy(out=xt[:, q * 512:(q + 1) * 512], in_=bk[:]), r=[bk.k], w=[xt.k])
            P.op("act", lambda e, xt=xt: e.activation(out=junk[:], in_=xt[:], func=AF.Square, accum_out=ss[:, 0:1]), r=[xt.k], w=[junk.k, ss.k])
            P.op("act", lambda e: e.activation(out=ss[:, 1:2], in_=ss[:, 0:1], func=AF.Sqrt, scale=1.0 / D, bias=epsb[:]), r=[ss.k, epsb.k], w=[ss.k])
            P.op("dve", lambda e: e.reciprocal(out=ss[:, 1:2], in_=ss[:, 1:2]), r=[ss.k], w=[ss.k])
            P.op("dve", lambda e, xt=xt, yt=yt: e.scalar_tensor_tensor(out=yt[:], in0=xt[:], scalar=ss[:, 1:2], in1=fw[:], op0=ALU.mult, op1=ALU.mult),
                 r=[xt.k, ss.k, fw.k], w=[yt.k])
            op_ = P.dma("sp", lambda e, yt=yt, t=t: e.dma_start(out=y_out[t * 128:(t + 1) * 128, :], in_=yt[:]), r=[yt.k])
            P.out_dmas.append(op_)
        S.close(final=True)

    plan = [("mod", stage_mod), ("t0", stage_t0)]
    for l in range(DEPTH):
        plan += [(f"n1_{l}", lambda l=l: stage_norm(l, 0)),
                 (f"win_{l}", lambda l=l: stage_win(l)),
                 (f"dn_{l}", lambda l=l: stage_dn(l)),
                 (f"mix_{l}", lambda l=l: stage_mix(l)),
                 (f"n2_{l}", lambda l=l: stage_norm(l, 1)),
                 (f"up_{l}", lambda l=l: stage_ffn_up(l)),
                 (f"down_{l}", lambda l=l: stage_ffn_down(l))]
    plan += [("fin", stage_final)]
    for name, fn in plan:
        if SKIP and name in SKIP:
            continue
        fn()
        if STOP_AFTER is not None and (name == STOP_AFTER or STOP_AFTER.startswith(name + ":")):
            break
    G.close(final=True)
    _CACHE["n_ops"] = P.n_total


SKIP = None


def _fm(v, nchunk):
    return np.ascontiguousarray(np.asarray(v, np.float32).reshape(nchunk, 128).T)


def make_in_maps(inp):
    f = lambda a: np.ascontiguousarray(np.asarray(a, dtype=np.float32))
    shared = {
        "consts": CONST_ARR, "poolinv": np.ascontiguousarray(POOL_INV.reshape(1, 1408)),
        "w_ada": f(inp["w_ada"]),
        "b_adaT": np.stack([_fm(inp["b_ada"][l], 96) for l in range(DEPTH)]),
        "n1T": np.stack([_fm(inp["norm1_w"][l], 16) for l in range(DEPTH)]),
        "n2T": np.stack([_fm(inp["norm2_w"][l], 16) for l in range(DEPTH)]),
        "w_in": f(inp["w_in"]),
        "convqT": np.ascontiguousarray(np.asarray(inp["conv_qkv"], np.float32).reshape(DEPTH, 5, 24, 128).transpose(0, 3, 2, 1)),
        "a_log": f(np.asarray(inp["a_log"]).reshape(DEPTH, 16)),
        "dt_bias": f(np.asarray(inp["dt_bias"]).reshape(DEPTH, 16)),
        "dnwT": np.ascontiguousarray(np.asarray(inp["dn_norm_w"], np.float32).T),
        "sgu_w": f(inp["sgu_norm_w"]),
        "wspT": np.ascontiguousarray(np.asarray(inp["w_spatial"], np.float32).transpose(0, 3, 1, 2)),
        "b_sp": f(np.asarray(inp["b_spatial"]).reshape(DEPTH, 512)),
        "pool_w": f(inp["pool_w"]),
        "pscT": np.stack([_fm(inp["pool_scale"][l], 8) for l in range(DEPTH)]),
        "p_a": f(inp["p_a"]), "p_b": f(inp["p_b"]), "p_c": f(inp["p_c"]),
        "w_out": f(inp["w_out"]), "w_up": f(inp["w_up"]),
        "cffT": np.ascontiguousarray(np.asarray(inp["conv_ffn"], np.float32).reshape(DEPTH, 9, 88, 128).transpose(0, 3, 2, 1)),
        "w_down": f(inp["w_down"]),
        "fnw": f(np.asarray(inp["final_norm_w"]).reshape(1, D)),
    }
    xs = np.asarray(inp["x_sample"], np.float32)
    xp = np.asarray(inp["x_prompt"], np.float32)
    sd = np.asarray(inp["state_delta"], np.float32)
    cc = np.asarray(inp["c"], np.float32)
    cctx = np.asarray(inp["c_ctx"], np.float32)
    maps = []
    for i in range(8):
        m = dict(shared)
        m["x"] = np.ascontiguousarray(np.concatenate([xs[i], xp[2 * i], xp[2 * i + 1]], axis=0))
        m["s0"] = np.ascontiguousarray(sd[i])
        cond = np.stack([cc[i], cctx], axis=0)
        m["condT"] = np.ascontiguousarray(cond.reshape(2, 16, 128).transpose(2, 1, 0))
        maps.append(m)
    return maps


def kernel(**inputs):
    if "nc" not in _CACHE:
        _CACHE["nc"] = build()
    nc = _CACHE["nc"]
    maps = make_in_maps(inputs)
    res = run_bass_kernel_spmd(nc, maps, core_ids=list(range(8)))
    _CACHE["res"] = res
    y_s = np.stack([res.results[i]["y"][0:LS] for i in range(8)], axis=0)
    y_p = np.stack([res.results[i]["y"][LS + s * LP:LS + (s + 1) * LP] for i in range(8) for s in range(2)], axis=0)
    ns = np.concatenate([res.results[i]["ns"] for i in range(8)], axis=0)
    return (np.ascontiguousarray(y_p, dtype=np.float32), np.ascontiguousarray(y_s, dtype=np.float32),
            np.ascontiguousarray(ns, dtype=np.float32))
```
